# Optimizing a Trainium2 kernel written in Bass

```python
import math
import jax, jax.numpy as jnp
from jax import lax
import numpy as np

D_MODEL = 1024
BATCH = 2
SEQ = 8192
DEPTH = 1

SSD_EXPAND = 2
D_INNER = SSD_EXPAND * D_MODEL
SSD_HEAD_DIM = 64
SSD_HEADS = D_INNER // SSD_HEAD_DIM
SSD_GROUPS = 4
D_STATE = 128
CONV_WIDTH = 4
SSD_CHUNK = 128
D_XBC = D_INNER + 2 * SSD_GROUPS * D_STATE
DIFF_HEADS = 8
DIFF_HEAD_DIM = 64
DIFF_V_DIM = 2 * DIFF_HEAD_DIM
DIFF_QK = DIFF_HEADS * 2 * DIFF_HEAD_DIM
DIFF_V = DIFF_HEADS * DIFF_V_DIM
Q_BLOCK = 128
MEM_LEN = 256
MEM_HEADS = 4
MEM_HEAD_DIM = D_MODEL // MEM_HEADS
PEER_HEADS = 8
PEER_N_KEYS = 128
PEER_N_EXPERTS = PEER_N_KEYS * PEER_N_KEYS
PEER_D_KEY = 256
PEER_HALF = PEER_D_KEY // 2
PEER_TOPK = 16
PEER_BLOCK = 128
ALPHA = (2 * DEPTH) ** 0.25
BETA = (8 * DEPTH) ** -0.25
LN_EPS = 1e-5
RMS_EPS = 1e-5
D_IN_PROJ = D_INNER + D_XBC + SSD_HEADS + 2 * DIFF_QK + DIFF_V + 2 * D_MODEL

kernel_name = "hybrid_ssd_diffattn_peer_block"


def _split_cols(h):
    sizes = (D_INNER, D_XBC, SSD_HEADS, DIFF_QK, DIFF_QK, DIFF_V, D_MODEL, D_MODEL)
    offs = np.cumsum(sizes)[:-1].tolist()
    return jnp.split(h, offs, axis=-1)


def layer_norm(x, g, b):
    xf = x.astype(jnp.float32)
    mu = jnp.mean(xf, axis=-1, keepdims=True)
    var = jnp.mean(jnp.square(xf - mu), axis=-1, keepdims=True)
    return ((xf - mu) * lax.rsqrt(var + LN_EPS) * g + b).astype(x.dtype)


def rms_norm(x, w):
    xf = x.astype(jnp.float32)
    xf = xf * lax.rsqrt(jnp.mean(jnp.square(xf), axis=-1, keepdims=True) + RMS_EPS)
    return (xf * w).astype(x.dtype)


def causal_dwconv(x, w, b):
    y = lax.conv_general_dilated(
        x, w[:, None, :], window_strides=(1,), padding=[(CONV_WIDTH - 1, 0)],
        dimension_numbers=("NWC", "WIO", "NWC"), feature_group_count=x.shape[-1])
    return y + b


def ssd_chunked(X, dtA, Bm, Cm):
    b, l, h, p = X.shape
    g, n = Bm.shape[2], Bm.shape[3]
    e = h // g
    c = l // SSD_CHUNK
    L = SSD_CHUNK
    X = X.reshape(b, c, L, g, e, p)
    A = dtA.astype(jnp.float32).reshape(b, c, L, g, e)
    Bm = Bm.reshape(b, c, L, g, n)
    Cm = Cm.reshape(b, c, L, g, n)
    Acs = jnp.cumsum(A, axis=2)
    AcsT = jnp.moveaxis(Acs, 2, -1)
    seg = AcsT[..., :, None] - AcsT[..., None, :]
    tril = jnp.tril(jnp.ones((L, L), dtype=bool))
    decay = jnp.exp(jnp.where(tril, seg, -jnp.inf))
    CB = jnp.einsum("bclgn,bcsgn->bcgls", Cm, Bm)
    M = CB[:, :, :, None] * decay
    y_diag = jnp.einsum("bcgels,bcsgep->bclgep", M, X)
    decay_states = jnp.exp(Acs[:, :, -1:] - Acs)
    states = jnp.einsum("bclgn,bclgep->bcgepn", Bm, X * decay_states[..., None])
    chunk_decay = jnp.exp(Acs[:, :, -1])

    def step(h_prev, inp):
        st, dec = inp
        return h_prev * dec[..., None, None] + st, h_prev

    h0 = jnp.zeros_like(states[:, 0])
    _, prev = lax.scan(step, h0, (jnp.moveaxis(states, 1, 0), jnp.moveaxis(chunk_decay, 1, 0)))
    prev = jnp.moveaxis(prev, 0, 1)
    y_off = jnp.einsum("bclgn,bcgepn->bclgep", Cm, prev) * jnp.exp(Acs)[..., None]
    return (y_diag + y_off).reshape(b, l, h, p)


def mamba2_branch(z, xbc, dt_raw, conv_w, conv_b, dt_bias, a_log, d_skip, norm_w):
    bsz, seq, _ = z.shape
    xbc = jax.nn.silu(causal_dwconv(xbc, conv_w, conv_b))
    xs, bm, cm = jnp.split(xbc, [D_INNER, D_INNER + SSD_GROUPS * D_STATE], axis=-1)
    dt = jax.nn.softplus(dt_raw.astype(jnp.float32) + dt_bias.astype(jnp.float32))
    a = -jnp.exp(a_log.astype(jnp.float32))
    xh = xs.reshape(bsz, seq, SSD_HEADS, SSD_HEAD_DIM)
    y = ssd_chunked(xh * dt[..., None], dt * a,
                    bm.reshape(bsz, seq, SSD_GROUPS, D_STATE),
                    cm.reshape(bsz, seq, SSD_GROUPS, D_STATE))
    y = y + d_skip[:, None] * xh
    y = y.reshape(bsz, seq, D_INNER) * jax.nn.silu(z)
    yg = y.astype(jnp.float32).reshape(bsz, seq, SSD_GROUPS, D_INNER // SSD_GROUPS)
    yg = yg * lax.rsqrt(jnp.mean(jnp.square(yg), axis=-1, keepdims=True) + RMS_EPS)
    return (yg.reshape(bsz, seq, D_INNER) * norm_w).astype(z.dtype)


def diff_attention(q, k, v, lam, lambda_init, subln_w):
    bsz, seq, _ = q.shape
    nb = seq // Q_BLOCK
    qb = q.reshape(bsz, nb, Q_BLOCK, DIFF_HEADS, 2, DIFF_HEAD_DIM).transpose(1, 0, 3, 4, 2, 5)
    qb = qb * (DIFF_HEAD_DIM ** -0.5)
    kt = k.reshape(bsz, seq, DIFF_HEADS, 2, DIFF_HEAD_DIM).transpose(0, 2, 3, 1, 4)
    vt = v.reshape(bsz, seq, DIFF_HEADS, DIFF_V_DIM).transpose(0, 2, 1, 3)
    key_pos = jnp.arange(seq)

    def one_block(args):
        qblk, blk = args
        s = jnp.einsum("bhmqd,bhmkd->bhmqk", qblk, kt).astype(jnp.float32)
        q_pos = blk * Q_BLOCK + jnp.arange(Q_BLOCK)
        s = jnp.where(key_pos[None, :] <= q_pos[:, None], s, -jnp.inf)
        p = jax.nn.softmax(s, axis=-1)
        att = p[:, :, 0] - lam * p[:, :, 1]
        return jnp.einsum("bhqk,bhkv->bhqv", att.astype(vt.dtype), vt)

    o = lax.map(one_block, (qb, jnp.arange(nb)))
    o = o.transpose(1, 0, 3, 2, 4).reshape(bsz, seq, DIFF_HEADS, DIFF_V_DIM)
    o = rms_norm(o, subln_w) * (1.0 - lambda_init)
    return o.reshape(bsz, seq, DIFF_V)


def memory_cross_attention(x, mem, wq, wk, wv, wo):
    bsz, seq, _ = x.shape
    m = mem.shape[1]
    q = (x @ wq).reshape(bsz, seq, MEM_HEADS, MEM_HEAD_DIM)
    k = (mem @ wk).reshape(bsz, m, MEM_HEADS, MEM_HEAD_DIM)
    v = (mem @ wv).reshape(bsz, m, MEM_HEADS, MEM_HEAD_DIM)
    s = jnp.einsum("bshd,bmhd->bhsm", q, k).astype(jnp.float32) * (MEM_HEAD_DIM ** -0.5)
    p = jax.nn.softmax(s, axis=-1).astype(x.dtype)
    o = jnp.einsum("bhsm,bmhd->bshd", p, v).reshape(bsz, seq, D_MODEL)
    return o @ wo


def peer_ffn(x, wq, sub_keys, peer_u, peer_v):
    bsz, seq, _ = x.shape
    q = (x @ wq).reshape(bsz, seq, PEER_HEADS, 2, PEER_HALF)
    s = jnp.einsum("bshtd,htkd->bshtk", q, sub_keys).astype(jnp.float32)
    top_s, top_i = lax.top_k(s, PEER_TOPK)
    cand_s = (top_s[..., 0, :, None] + top_s[..., 1, None, :]).reshape(bsz, seq, PEER_HEADS, PEER_TOPK * PEER_TOPK)
    cand_i = (top_i[..., 0, :, None] * PEER_N_KEYS + top_i[..., 1, None, :]).reshape(bsz, seq, PEER_HEADS, PEER_TOPK * PEER_TOPK)
    best_s, best_j = lax.top_k(cand_s, PEER_TOPK)
    idx = jnp.take_along_axis(cand_i, best_j, axis=-1)
    gate = jax.nn.softmax(best_s, axis=-1).astype(x.dtype)
    nb = seq // PEER_BLOCK

    def blockify(t):
        return t.reshape(bsz, nb, PEER_BLOCK, *t.shape[2:]).swapaxes(0, 1)

    def one_block(args):
        xb, ib, gb = args
        u = jnp.take(peer_u, ib, axis=0)
        hid = jnp.einsum("bqhkd,bqd->bqhk", u, xb)
        w = gb * jax.nn.gelu(hid, approximate=False)
        vv = jnp.take(peer_v, ib, axis=0)
        return jnp.einsum("bqhk,bqhkd->bqd", w, vv)

    y = lax.map(one_block, (blockify(x), blockify(idx), blockify(gate)))
    return y.swapaxes(0, 1).reshape(bsz, seq, D_MODEL)


def setup_inputs(seed: int = 0) -> dict:
    key = jax.random.key(seed)
    ks = iter(jax.random.split(key, 40))

    def nrm(shape, std):
        return jax.random.normal(next(ks), shape, jnp.float32) * std

    def near_one(shape):
        return 1.0 + 0.01 * jax.random.normal(next(ks), shape, jnp.float32)

    dt0 = jnp.exp(jax.random.uniform(next(ks), (DEPTH, SSD_HEADS), jnp.float32,
                                     math.log(1e-3), math.log(1e-1)))
    dt_bias = dt0 + jnp.log(-jnp.expm1(-dt0))
    a_log = jnp.log(jax.random.uniform(next(ks), (DEPTH, SSD_HEADS), jnp.float32, 1.0, 16.0))
    return {
        "x": nrm((BATCH, SEQ, D_MODEL), 1.0),
        "mem": nrm((BATCH, MEM_LEN, D_MODEL), 1.0),
        "w_in": nrm((DEPTH, D_MODEL, D_IN_PROJ), D_MODEL ** -0.5),
        "conv_w": nrm((DEPTH, CONV_WIDTH, D_XBC), CONV_WIDTH ** -0.5),
        "conv_b": nrm((DEPTH, D_XBC), 0.01),
        "dt_bias": dt_bias,
        "a_log": a_log,
        "d_skip": near_one((DEPTH, SSD_HEADS)),
        "ssd_norm_w": near_one((DEPTH, D_INNER)),
        "w_ssd_br": nrm((DEPTH, D_INNER, D_MODEL), D_INNER ** -0.5),
        "lam_q": nrm((DEPTH, 2, DIFF_HEAD_DIM), 0.1),
        "lam_k": nrm((DEPTH, 2, DIFF_HEAD_DIM), 0.1),
        "subln_w": near_one((DEPTH, DIFF_V_DIM)),
        "w_diff_br": nrm((DEPTH, DIFF_V, D_MODEL), DIFF_V ** -0.5),
        "gate_bias": nrm((DEPTH, 2, D_MODEL), 0.01),
        "w_o": nrm((DEPTH, D_MODEL, D_MODEL), BETA * D_MODEL ** -0.5),
        "ln1_g": near_one((DEPTH, D_MODEL)),
        "ln1_b": nrm((DEPTH, D_MODEL), 0.01),
        "w_cq": nrm((DEPTH, D_MODEL, D_MODEL), D_MODEL ** -0.5),
        "w_ck": nrm((DEPTH, D_MODEL, D_MODEL), D_MODEL ** -0.5),
        "w_cv": nrm((DEPTH, D_MODEL, D_MODEL), D_MODEL ** -0.5),
        "w_co": nrm((DEPTH, D_MODEL, D_MODEL), BETA * D_MODEL ** -0.5),
        "ln2_g": near_one((DEPTH, D_MODEL)),
        "ln2_b": nrm((DEPTH, D_MODEL), 0.01),
        "w_pq": nrm((DEPTH, D_MODEL, PEER_HEADS * PEER_D_KEY), D_MODEL ** -0.5),
        "sub_keys": nrm((DEPTH, PEER_HEADS, 2, PEER_N_KEYS, PEER_HALF), PEER_HALF ** -0.5),
        "peer_u": nrm((DEPTH, PEER_N_EXPERTS, D_MODEL), D_MODEL ** -0.5),
        "peer_v": nrm((DEPTH, PEER_N_EXPERTS, D_MODEL), BETA * PEER_HEADS ** -0.5),
        "ln3_g": near_one((DEPTH, D_MODEL)),
        "ln3_b": nrm((DEPTH, D_MODEL), 0.01),
    }


def reference(x, mem, w_in, conv_w, conv_b, dt_bias, a_log, d_skip, ssd_norm_w, w_ssd_br,
              lam_q, lam_k, subln_w, w_diff_br, gate_bias, w_o, ln1_g, ln1_b,
              w_cq, w_ck, w_cv, w_co, ln2_g, ln2_b, w_pq, sub_keys, peer_u, peer_v,
              ln3_g, ln3_b):
    for l in range(DEPTH):
        lambda_init = 0.8 - 0.6 * math.exp(-0.3 * l)
        h = x @ w_in[l]
        z, xbc, dt_raw, q, k, v, g_ssd, g_att = _split_cols(h)
        y_ssd = mamba2_branch(z, xbc, dt_raw, conv_w[l], conv_b[l], dt_bias[l], a_log[l],
                              d_skip[l], ssd_norm_w[l]) @ w_ssd_br[l]
        lq = lam_q[l].astype(jnp.float32)
        lk = lam_k[l].astype(jnp.float32)
        lam = jnp.exp(jnp.sum(lq[0] * lk[0])) - jnp.exp(jnp.sum(lq[1] * lk[1])) + lambda_init
        y_att = diff_attention(q, k, v, lam, lambda_init, subln_w[l]) @ w_diff_br[l]
        gate_a = jax.nn.sigmoid(g_ssd + gate_bias[l, 0])
        gate_b = jax.nn.sigmoid(g_att + gate_bias[l, 1])
        mix = (gate_a * y_ssd + gate_b * y_att) @ w_o[l]
        x = layer_norm(ALPHA * x + mix, ln1_g[l], ln1_b[l])
        x = layer_norm(ALPHA * x + memory_cross_attention(x, mem, w_cq[l], w_ck[l], w_cv[l], w_co[l]),
                       ln2_g[l], ln2_b[l])
        x = layer_norm(ALPHA * x + peer_ffn(x, w_pq[l], sub_keys[l], peer_u[l], peer_v[l]),
                       ln3_g[l], ln3_b[l])
    return x
```

```python
from contextlib import ExitStack
import numpy as np
import concourse.bass as bass
import concourse.mybir as mybir
from concourse.bass_utils import run_bass_kernel_spmd

F32 = mybir.dt.float32
BF16 = mybir.dt.bfloat16
I32 = mybir.dt.int32
U32 = mybir.dt.uint32
AF = mybir.ActivationFunctionType
ALU = mybir.AluOpType
AX = mybir.AxisListType

ALPHA = 2.0 ** 0.25
LN_EPS = 1e-5
RMS_EPS = 1e-5
LAMBDA_INIT = 0.8 - 0.6
NR = 920
NEG = -30000.0


class Buf:
    __slots__ = ("w", "r")

    def __init__(self):
        self.w = None
        self.r = {}


class FW:
    EPOCH = 12000

    def __init__(self, nc, stack):
        self.nc = nc
        self.stack = stack
        self.nsem = 0
        self.eng = {}
        for name, e in (("pe", nc.tensor), ("act", nc.scalar), ("dve", nc.vector),
                        ("pool", nc.gpsimd), ("sp", nc.sync)):
            self.eng[name] = dict(e=e, sem=None, cnt=0, seen={})
            self._new_sem(name)
        self.dq = {}

    def _mk_sem(self, tag):
        self.nsem += 1
        return self.stack.enter_context(self.nc.semaphore(f"{tag}_{self.nsem}"))

    def _new_sem(self, name):
        E = self.eng[name]
        E["sem"] = self._mk_sem("s" + name)
        E["cnt"] = 0

    def _wait(self, E, tok):
        sem, val = tok
        k = id(sem)
        if E["seen"].get(k, 0) >= val:
            return
        E["e"].wait_ge(sem, val)
        E["seen"][k] = val

    def _deps(self, E, reads, writes, skip_self=False):
        toks = {}

        def add(t):
            if t is None:
                return
            k = id(t[0])
            if k not in toks or toks[k][1] < t[1]:
                toks[k] = t
        for b in reads:
            add(b.w)
        for b in writes:
            add(b.w)
            for t in b.r.values():
                add(t)
        for t in toks.values():
            if skip_self and t[0] is E["sem"]:
                continue
            self._wait(E, t)

    def _mark(self, tok, reads, writes):
        k = id(tok[0])
        for b in reads:
            b.r[k] = tok
        for b in writes:
            b.w = tok
            b.r = {}

    def op(self, name, fn, reads=(), writes=(), skip_self=False):
        E = self.eng[name]
        if E["cnt"] >= self.EPOCH:
            self._new_sem(name)
        self._deps(E, reads, writes, skip_self)
        ins = fn(E["e"])
        E["cnt"] += 1
        ins.then_inc(E["sem"], 1)
        tok = (E["sem"], E["cnt"])
        self._mark(tok, reads, writes)
        return tok

    def dma(self, q, fn, reads=(), writes=(), nslots=8, inc=16):
        E = self.eng[q]
        D = self.dq.setdefault(q, dict(slots=[], i=0))
        if len(D["slots"]) < nslots:
            D["slots"].append(dict(sem=self._mk_sem("d" + q), val=0))
            s = D["slots"][-1]
        else:
            s = D["slots"][D["i"] % nslots]
        D["i"] += 1
        if s["val"] >= 16 * 900:
            self._wait(E, (s["sem"], s["val"]))
            s["sem"] = self._mk_sem("d" + q)
            s["val"] = 0
        if s["val"] > 0:
            self._wait(E, (s["sem"], s["val"]))
        self._deps(E, reads, writes)
        ins = fn(E["e"])
        s["val"] += inc
        ins.then_inc(s["sem"], inc)
        tok = (s["sem"], s["val"])
        self._mark(tok, reads, writes)
        return tok

    def barrier(self):
        toks = []
        for E in self.eng.values():
            if E["cnt"] > 0:
                toks.append((E["sem"], E["cnt"]))
        for D in self.dq.values():
            for s in D["slots"]:
                if s["val"] > 0:
                    toks.append((s["sem"], s["val"]))
        for E in self.eng.values():
            for t in toks:
                if t[0] is E["sem"]:
                    continue
                self._wait(E, t)

    def drain(self, q):
        E = self.eng[q]
        for s in self.dq.get(q, dict(slots=[]))["slots"]:
            if s["val"] > 0:
                self._wait(E, (s["sem"], s["val"]))


class _Stop(Exception):
    pass


def build(DBG=False, stop=99):
    try:
        return _build(DBG, stop)
    except _Stop as s:
        return s.args[0]


def _build(DBG=False, stop=99):
    nc = bass.Bass("TRN2", target_bir_lowering=False)

    def din(name, shape, dt=F32):
        return nc.dram_tensor(name, shape, dt, kind="ExternalInput").ap()

    xb = din("xb", [8192, 1024]); xown = din("xown", [2048, 1024]); memb = din("memb", [256, 1024])
    w1 = din("w1", [1024, 2056]); wg = din("wg", [1024, 2048])
    cw = din("cw", [128, 6, 4]); cbv = din("cbv", [128, 6]); rowp = din("rowp", [128, NR])
    lnp = din("lnp", [128, 6, 1024]); gbias = din("gbias", [128, 16])
    w_ssd = din("w_ssd_br", [2048, 1024]); w_diff = din("w_diff_br", [1024, 1024]); w_o = din("w_o", [1024, 1024])
    w_cq = din("w_cq", [1024, 1024]); w_ck = din("w_ck", [1024, 1024]); w_cv = din("w_cv", [1024, 1024]); w_co = din("w_co", [1024, 1024])
    w_pq = din("w_pq", [1024, 2048]); subk = din("sub_keys", [16, 128, 128])
    peer_uv = din("peer_uv", [16384, 2048])
    idq = din("idq", [128, 24, 8], I32)
    out = nc.dram_tensor("out", [2048, 1024], F32, kind="ExternalOutput").ap()
    XB = nc.dram_tensor("xbuf", [768, 8192], BF16).ap()
    GX = nc.dram_tensor("gxbuf", [3072, 8192], BF16).ap()
    X1D = nc.dram_tensor("x1d", [2048, 1024], F32).ap()
    X2D = nc.dram_tensor("x2d", [2048, 1024], F32).ap()
    IDXD = nc.dram_tensor("idxd", [128, 2048], I32).ap()
    UVB = nc.dram_tensor("uvb", [16384, 2048], BF16).ap()
    GATED = nc.dram_tensor("gated", [128, 2048], F32).ap()
    if DBG:
        d_xb = nc.dram_tensor("d_xb", [768, 8192], BF16, kind="ExternalOutput").ap()
        d_x1 = nc.dram_tensor("d_x1", [2048, 1024], F32, kind="ExternalOutput").ap()
        d_x2 = nc.dram_tensor("d_x2", [2048, 1024], F32, kind="ExternalOutput").ap()
        d_att = nc.dram_tensor("d_att", [128, 1344], F32, kind="ExternalOutput").ap()
        d_pt = nc.dram_tensor("d_pt", [128, 512], BF16, kind="ExternalOutput").ap()
        d_v = nc.dram_tensor("d_v", [128, 130], BF16, kind="ExternalOutput").ap()
        d_q = nc.dram_tensor("d_q", [128, 512], BF16, kind="ExternalOutput").ap()
        d_k = nc.dram_tensor("d_k", [128, 512], BF16, kind="ExternalOutput").ap()

    with ExitStack() as st0:
        fw = FW(nc, st0)
        op = fw.op

        def ckpt(k):
            if stop == k:
                fw.barrier()
                raise _Stop(nc)
        bXB, bGX, bX1D, bX2D = Buf(), Buf(), Buf(), Buf()
        bXBs = Buf()

        uid = [0]

        def SB(st, name, shape, dt):
            uid[0] += 1
            return st.enter_context(nc.sbuf_tensor(f"{name}_{uid[0]}", shape, dt)), Buf()

        def PSB(st, name, shape, dt):
            uid[0] += 1
            return st.enter_context(nc.psum_tensor(f"{name}_{uid[0]}", shape, dt)), Buf()

        def mm(o, lhsT, rhs, start, stop, reads, writes):
            op("pe", lambda e: e.matmul(o, lhsT=lhsT, rhs=rhs, start=start, stop=stop), reads, writes, skip_self=True)

        def tr(o, in_, ident, reads, writes):
            op("pe", lambda e: e.transpose(out=o, in_=in_, identity=ident), reads, writes, skip_self=True)

        rr = [0]

        def cast(o, i, reads, writes, engs=("dve", "pool")):
            n = engs[rr[0] % len(engs)]
            rr[0] += 1
            if n == "act":
                op(n, lambda e: e.copy(out=o, in_=i), reads, writes)
            else:
                op(n, lambda e: e.tensor_copy(out=o, in_=i), reads, writes)

        ones_f, b_c = SB(st0, "ones_f", [128, 128], F32)
        ones_bf, _ = SB(st0, "ones_bf", [128, 128], BF16)
        ident_bf, _ = SB(st0, "ident_bf", [128, 128], BF16)
        ident_f, _ = SB(st0, "ident_f", [128, 128], F32)
        tri_f, _ = SB(st0, "tri_f", [128, 128], F32)
        tri_bf, _ = SB(st0, "tri_bf", [128, 128], BF16)
        negm_f, _ = SB(st0, "negm_f", [128, 128], F32)
        negc, _ = SB(st0, "negc", [128, 128], F32)
        sel_bf, _ = SB(st0, "sel_bf", [128, 2, 128], BF16)
        rp, _ = SB(st0, "rp", [128, NR], F32)
        lnt, blnt = SB(st0, "lnt", [128, 2, 1024], F32)
        small, b_sm = SB(st0, "small", [128, 64], F32)
        C = [b_c]
        op("pool", lambda e: e.memset(ones_f[:], 1.0), [], C)
        op("pool", lambda e: e.memset(ones_bf[:], 1.0), [], C)
        op("pool", lambda e: e.memset(negc[:], NEG), [], C)
        op("pool", lambda e: e.memset(sel_bf[:], 0.0), [], C)
        op("pool", lambda e: e.memset(sel_bf[0:64, 0, :], 1.0), C, C)
        op("pool", lambda e: e.memset(sel_bf[64:128, 1, :], 1.0), C, C)
        op("pool", lambda e: e.affine_select(out=ident_bf[:], in_=ones_f[:], pattern=[[-1, 128]], compare_op=ALU.is_equal, fill=0.0, base=0, channel_multiplier=1), C, C)
        op("pool", lambda e: e.affine_select(out=ident_f[:], in_=ones_f[:], pattern=[[-1, 128]], compare_op=ALU.is_equal, fill=0.0, base=0, channel_multiplier=1), C, C)
        op("pool", lambda e: e.affine_select(out=tri_f[:], in_=ones_f[:], pattern=[[1, 128]], compare_op=ALU.is_ge, fill=0.0, base=0, channel_multiplier=-1), C, C)
        op("pool", lambda e: e.affine_select(out=tri_bf[:], in_=ones_f[:], pattern=[[1, 128]], compare_op=ALU.is_ge, fill=0.0, base=0, channel_multiplier=-1), C, C)
        op("pool", lambda e: e.affine_select(out=negm_f[:], in_=negc[:], pattern=[[-1, 128]], compare_op=ALU.is_gt, fill=0.0, base=0, channel_multiplier=1), C, C)
        fw.dma("sp", lambda e: e.dma_start(out=rp[:], in_=rowp), [], C)
        DTB, ALOG, DSK, NORMW, SUBLN, LAMQ, LAMK = 0, 8, 16, 24, 536, 664, 792
        arow, _ = SB(st0, "arow", [128, 8], F32)
        subs, _ = SB(st0, "subs", [128, 128], F32)
        lamt, _ = SB(st0, "lamt", [128, 128], F32)
        op("act", lambda e: e.activation(out=arow[:], in_=rp[:, ALOG:ALOG + 8], func=AF.Exp), C, C)
        op("dve", lambda e: e.tensor_scalar_mul(out=arow[:], in0=arow[:], scalar1=-1.0), C, C)
        op("dve", lambda e: e.tensor_scalar_mul(out=subs[:], in0=rp[:, SUBLN:SUBLN + 128], scalar1=1.0 - LAMBDA_INIT), C, C)
        op("dve", lambda e: e.tensor_tensor(out=lamt[:], in0=rp[:, LAMQ:LAMQ + 128], in1=rp[:, LAMK:LAMK + 128], op=ALU.mult), C, C)
        op("dve", lambda e: e.reduce_sum(out=small[:, 0:1], in_=lamt[:, 0:64], axis=AX.X), C, [b_sm])
        op("dve", lambda e: e.reduce_sum(out=small[:, 1:2], in_=lamt[:, 64:128], axis=AX.X), C, [b_sm])
        op("act", lambda e: e.activation(out=small[:, 2:4], in_=small[:, 0:2], func=AF.Exp), [b_sm], [b_sm])
        op("dve", lambda e: e.tensor_tensor(out=small[:, 4:5], in0=small[:, 3:4], in1=small[:, 2:3], op=ALU.subtract), [b_sm], [b_sm])
        op("dve", lambda e: e.tensor_scalar_add(out=small[:, 5:6], in0=small[:, 4:5], scalar1=-LAMBDA_INIT), [b_sm], [b_sm])
        neglam = small[:, 5:6]

        def load_w_bf(st, name, src, ncols, c0=0, nk=8):
            wt, bw = SB(st, name, [128, nk, ncols], BF16)
            with ExitStack() as s2:
                cw_ = min(ncols, 1024)
                stg = [SB(s2, f"{name}_stg{i}", [128, cw_], F32) for i in range(2)]
                n = 0
                for kc in range(nk):
                    for cc in range(0, ncols, cw_):
                        w_ = min(cw_, ncols - cc)
                        t_, b_ = stg[n % 2]
                        n += 1
                        fw.dma("sp", lambda e: e.dma_start(out=t_[:, 0:w_], in_=src[kc * 128:(kc + 1) * 128, c0 + cc:c0 + cc + w_]), [], [b_])
                        cast(wt[:, kc, cc:cc + w_], t_[:, 0:w_], [b_], [bw])
                fw.barrier()
            return wt, bw

        def load_xT(st_bufs, src, row0, xT, bxT, PT, bPT, keep32=None, bkeep=None, cast_engs=("dve", "pool")):
            xs32, bxs32, xsbf, bxsbf = st_bufs
            for j in range(4):
                dst32 = keep32[:, j, :] if keep32 is not None else xs32[:]
                b32 = bkeep if keep32 is not None else bxs32
                fw.dma("sp", lambda e: e.dma_start(out=dst32, in_=src[row0 + j * 128:row0 + (j + 1) * 128, :]), [], [b32])
                cast(xsbf[:], dst32, [b32], [bxsbf], engs=cast_engs)
                for kc in range(8):
                    tr(PT[:, kc * 128:(kc + 1) * 128], xsbf[:, kc * 128:(kc + 1) * 128], ident_bf[:], [bxsbf] + C, [bPT])
                op("act", lambda e: e.copy(out=xT[:, :, j * 128:(j + 1) * 128], in_=PT[:].rearrange("p (k t) -> p k t", k=8)), [bPT], [bxT])

        def layer_norm(r, br, li, o, bo, tmp, btmp, st6, bst6):
            for hh in range(2):
                op("dve", lambda e: e.bn_stats(out=st6[:, hh * 6:(hh + 1) * 6], in_=r[:, hh * 512:(hh + 1) * 512]), [br], [bst6])
            op("dve", lambda e: e.bn_aggr(out=st6[:, 12:14], in_=st6[:, 0:12]), [bst6], [bst6])
            op("dve", lambda e: e.tensor_scalar_add(out=st6[:, 14:15], in0=st6[:, 13:14], scalar1=LN_EPS), [bst6], [bst6])
            op("act", lambda e: e.activation(out=st6[:, 15:16], in_=st6[:, 14:15], func=AF.Sqrt), [bst6], [bst6])
            op("dve", lambda e: e.reciprocal(out=st6[:, 16:17], in_=st6[:, 15:16]), [bst6], [bst6])
            op("dve", lambda e: e.tensor_scalar(out=tmp[:], in0=r[:], scalar1=st6[:, 12:13], scalar2=st6[:, 16:17], op0=ALU.subtract, op1=ALU.mult), [br, bst6], [btmp])
            op("pool", lambda e: e.tensor_tensor(out=tmp[:], in0=tmp[:], in1=lnt[:, 0, :], op=ALU.mult), [btmp, blnt], [btmp])
            op("pool", lambda e: e.tensor_tensor(out=o[:], in0=tmp[:], in1=lnt[:, 1, :], op=ALU.add), [btmp, blnt], [bo])

        bUVB = Buf()
        with ExitStack() as st:
            w1a, bw1a = load_w_bf(st, "w1a", w1, 1288, 0)
            PA, bPA = PSB(st, "PA", [128, 512], F32); PB, bPB = PSB(st, "PB", [128, 512], F32)
            PS0, bPS0 = PSB(st, "PS0", [128, 512], F32); PS1, bPS1 = PSB(st, "PS1", [128, 512], F32)
            PY, bPY = PSB(st, "PY", [128, 512], F32); PYo, bPYo = PSB(st, "PYo", [128, 512], F32)
            PT, bPT = PSB(st, "PT", [128, 1024], BF16); PM, bPM = PSB(st, "PM", [128, 512], F32)
            xs32, bxs32 = SB(st, "xs32", [128, 1024], F32); xsbf, bxsbf = SB(st, "xsbf", [128, 1024], BF16)
            xT, bxT = SB(st, "xT", [128, 8, 512], BF16)
            cwt, bcw = SB(st, "cwt", [128, 6, 4], F32); cbt, _ = SB(st, "cbt", [128, 6], F32)
            fw.dma("sp", lambda e: e.dma_start(out=cwt[:], in_=cw), [], [bcw])
            fw.dma("sp", lambda e: e.dma_start(out=cbt[:], in_=cbv), [], [bcw])
            halo, bhalo = SB(st, "halo", [128, 6, 3], F32)
            op("pool", lambda e: e.memset(halo[:], 0.0), [], [bhalo])
            xbc1, bxbc1 = SB(st, "xbc1", [128, 515], F32)
            acc, bacc = SB(st, "acc", [128, 512], F32)
            fm, bfm = SB(st, "fm", [128, 6, 512], BF16)
            tm, btm = SB(st, "tm", [128, 4, 640], BF16)
            zs, bzs = SB(st, "zs", [128, 512], F32)
            sm, bsm = SB(st, "sm", [128, 96], F32)
            H, bH = SB(st, "H", [128, 512], F32); Hbf, bHbf = SB(st, "Hbf", [128, 512], BF16)
            op("pool", lambda e: e.memset(H[:], 0.0), [], [bH])
            op("pool", lambda e: e.memset(Hbf[:], 0.0), [], [bHbf])
            X, bX = SB(st, "X", [128, 512], BF16); Xd, bXd = SB(st, "Xd", [128, 512], BF16)
            Abc, bAbc = SB(st, "Abc", [128, 8, 128], F32); Eall, bE = SB(st, "Eall", [128, 8, 128], F32)
            CBT, bCBT = SB(st, "CBT", [128, 128], F32); MT, bMT = SB(st, "MT", [128, 8, 128], BF16)
            t1, bt1 = SB(st, "t1", [128, 512], F32); t2, bt2 = SB(st, "t2", [128, 512], F32); t3, bt3 = SB(st, "t3", [128, 512], F32)
            yn, byn = SB(st, "yn", [128, 512], BF16); ygT, bygT = SB(st, "ygT", [128, 4, 512], BF16)
            v3 = lambda ap: ap.rearrange("p (h d) -> p h d", h=8)
            bc3 = lambda ap: ap.unsqueeze(2).broadcast_to([128, 8, 64])
            for g in range(16 if stop != 12 else 0):
                load_xT((xs32, bxs32, xsbf, bxsbf), xb, g * 512, xT, bxT, PT, bPT)
                for blk in range(6):
                    for kc in range(8):
                        mm(PA[:], w1a[:, kc, blk * 128:(blk + 1) * 128], xT[:, kc, :], kc == 0, kc == 7, [bw1a, bxT], [bPA])
                    op("pool", lambda e: e.tensor_copy(out=xbc1[:, 0:3], in_=halo[:, blk, :]), [bhalo], [bxbc1])
                    op("act", lambda e: e.copy(out=xbc1[:, 3:515], in_=PA[:]), [bPA], [bxbc1])
                    op("pool", lambda e: e.tensor_copy(out=halo[:, blk, :], in_=xbc1[:, 512:515]), [bxbc1], [bhalo])
                    op("dve", lambda e: e.tensor_scalar_mul(out=acc[:], in0=xbc1[:, 0:512], scalar1=cwt[:, blk, 0:1]), [bxbc1, bcw], [bacc])
                    for k in range(1, 4):
                        op("dve", lambda e: e.scalar_tensor_tensor(out=acc[:], in0=xbc1[:, k:k + 512], scalar=cwt[:, blk, k:k + 1], in1=acc[:], op0=ALU.mult, op1=ALU.add), [bxbc1, bcw, bacc], [bacc])
                    op("act", lambda e: e.activation(out=fm[:, blk, :], in_=acc[:], func=AF.Silu, bias=cbt[:, blk:blk + 1]), [bacc, bcw], [bfm])
                for j in range(4):
                    for blk in range(5):
                        tr(PT[:, blk * 128:(blk + 1) * 128], fm[:, blk, j * 128:(j + 1) * 128], ident_bf[:], [bfm] + C, [bPT])
                    op("act", lambda e: e.copy(out=tm[:, j, :], in_=PT[:, 0:640]), [bPT], [btm])
                for j in range(4):
                    ts = slice(j * 128, (j + 1) * 128)
                    for kc in range(8):
                        mm(PA[:], xT[:, kc, ts], w1a[:, kc, 768:1280], kc == 0, kc == 7, [bw1a, bxT], [bPA])
                    for kc in range(8):
                        mm(PM[:, 0:8], xT[:, kc, ts], w1a[:, kc, 1280:1288], kc == 0, kc == 7, [bw1a, bxT], [bPM])
                    op("act", lambda e: e.activation(out=zs[:], in_=PA[:], func=AF.Silu), [bPA], [bzs])
                    op("dve", lambda e: e.tensor_tensor(out=sm[:, 0:8], in0=PM[:, 0:8], in1=rp[:, DTB:DTB + 8], op=ALU.add), [bPM] + C, [bsm])
                    op("dve", lambda e: e.tensor_scalar_mul(out=sm[:, 8:16], in0=sm[:, 0:8], scalar1=-1.0), [bsm], [bsm])
                    op("dve", lambda e: e.tensor_tensor(out=sm[:, 8:16], in0=sm[:, 8:16], in1=sm[:, 0:8], op=ALU.max), [bsm], [bsm])
                    op("act", lambda e: e.activation(out=sm[:, 8:16], in_=sm[:, 8:16], func=AF.Exp, scale=-1.0), [bsm], [bsm])
                    op("act", lambda e: e.activation(out=sm[:, 8:16], in_=sm[:, 8:16], func=AF.Ln, bias=1.0), [bsm], [bsm])
                    op("dve", lambda e: e.tensor_scalar_max(out=sm[:, 0:8], in0=sm[:, 0:8], scalar1=0.0), [bsm], [bsm])
                    op("dve", lambda e: e.tensor_tensor(out=sm[:, 16:24], in0=sm[:, 0:8], in1=sm[:, 8:16], op=ALU.add), [bsm], [bsm])
                    op("dve", lambda e: e.tensor_tensor(out=sm[:, 24:32], in0=sm[:, 16:24], in1=arow[:], op=ALU.mult), [bsm] + C, [bsm])
                    mm(PM[:, 8:16], tri_f[:], sm[:, 24:32], True, True, [bsm] + C, [bPM])
                    mm(PM[:, 16:24], ones_f[:], sm[:, 24:32], True, True, [bsm] + C, [bPM])
                    op("dve", lambda e: e.tensor_scalar_mul(out=sm[:, 32:40], in0=PM[:, 8:16], scalar1=-1.0), [bPM], [bsm])
                    op("act", lambda e: e.activation(out=sm[:, 40:48], in_=PM[:, 8:16], func=AF.Exp), [bPM], [bsm])
                    op("dve", lambda e: e.tensor_tensor(out=sm[:, 48:56], in0=PM[:, 16:24], in1=sm[:, 32:40], op=ALU.add), [bPM, bsm], [bsm])
                    op("act", lambda e: e.activation(out=sm[:, 48:56], in_=sm[:, 48:56], func=AF.Exp), [bsm], [bsm])
                    op("act", lambda e: e.activation(out=sm[:, 56:64], in_=PM[:, 16:24], func=AF.Exp), [bPM], [bsm])
                    op("dve", lambda e: e.tensor_tensor(out=v3(X[:]), in0=v3(tm[:, j, 0:512]), in1=bc3(sm[:, 16:24]), op=ALU.mult), [btm, bsm], [bX])
                    op("pool", lambda e: e.tensor_tensor(out=v3(Xd[:]), in0=v3(X[:]), in1=bc3(sm[:, 48:56]), op=ALU.mult), [bX, bsm], [bXd])
                    op("dve", lambda e: e.tensor_copy(out=Abc[:], in_=sm[:, 24:32].unsqueeze(2).broadcast_to([128, 8, 128])), [bsm], [bAbc])
                    for half in range(2):
                        PSh, bPSh = (PS0, bPS0) if half == 0 else (PS1, bPS1)
                        for hq in range(4):
                            h = half * 4 + hq
                            mm(PSh[:, hq * 128:(hq + 1) * 128], Abc[:, h, :], tri_f[:], True, False, [bAbc] + C, [bPSh])
                            mm(PSh[:, hq * 128:(hq + 1) * 128], ident_f[:], negm_f[:], False, True, C, [bPSh])
                        for hq in range(4):
                            h = half * 4 + hq
                            op("act", lambda e: e.activation(out=Eall[:, h, :], in_=PSh[:, hq * 128:(hq + 1) * 128], func=AF.Exp, bias=sm[:, 32 + h:33 + h]), [bPSh, bsm], [bE])
                    mm(PB[:, 0:128], fm[:, 4, ts], fm[:, 5, ts], True, True, [bfm], [bPB])
                    op("dve", lambda e: e.tensor_copy(out=CBT[:], in_=PB[:, 0:128]), [bPB], [bCBT])
                    op("dve", lambda e: e.tensor_tensor(out=MT[:], in0=Eall[:], in1=CBT[:].unsqueeze(1).broadcast_to([128, 8, 128]), op=ALU.mult), [bE, bCBT], [bMT])
                    for h in range(8):
                        mm(PY[:, h * 64:(h + 1) * 64], MT[:, h, :], X[:, h * 64:(h + 1) * 64], True, True, [bMT, bX], [bPY])
                    mm(PYo[:], fm[:, 5, ts], Hbf[:], True, True, [bfm, bHbf], [bPYo])
                    op("dve", lambda e: e.tensor_tensor(out=v3(t1[:]), in0=v3(PYo[:]), in1=bc3(sm[:, 40:48]), op=ALU.mult), [bPYo, bsm], [bt1])
                    op("dve", lambda e: e.tensor_tensor(out=t2[:], in0=t1[:], in1=PY[:], op=ALU.add), [bt1, bPY], [bt2])
                    op("pool", lambda e: e.tensor_tensor(out=v3(t3[:]), in0=v3(tm[:, j, 0:512]), in1=bc3(rp[:, DSK:DSK + 8]), op=ALU.mult), [btm] + C, [bt3])
                    op("pool", lambda e: e.tensor_tensor(out=t3[:], in0=t3[:], in1=t2[:], op=ALU.add), [bt3, bt2], [bt3])
                    op("dve", lambda e: e.tensor_tensor(out=t1[:], in0=t3[:], in1=zs[:], op=ALU.mult), [bt3, bzs], [bt1])
                    op("pool", lambda e: e.memset(sm[:, 64:65], 0.0), [], [bsm])
                    op("act", lambda e: e.activation(out=t2[:], in_=t1[:], func=AF.Square, accum_out=sm[:, 64:65]), [bt1], [bt2, bsm])
                    op("dve", lambda e: e.tensor_scalar(out=sm[:, 65:66], in0=sm[:, 64:65], scalar1=1.0 / 512.0, scalar2=RMS_EPS, op0=ALU.mult, op1=ALU.add), [bsm], [bsm])
                    op("act", lambda e: e.activation(out=sm[:, 66:67], in_=sm[:, 65:66], func=AF.Sqrt), [bsm], [bsm])
                    op("dve", lambda e: e.reciprocal(out=sm[:, 67:68], in_=sm[:, 66:67]), [bsm], [bsm])
                    op("dve", lambda e: e.scalar_tensor_tensor(out=yn[:], in0=t1[:], scalar=sm[:, 67:68], in1=rp[:, NORMW:NORMW + 512], op0=ALU.mult, op1=ALU.mult), [bt1, bsm] + C, [byn])
                    for cb in range(4):
                        tr(PT[:, cb * 128:(cb + 1) * 128], yn[:, cb * 128:(cb + 1) * 128], ident_bf[:], [byn] + C, [bPT])
                    op("act", lambda e: e.copy(out=ygT[:, :, ts], in_=PT[:, 0:512].rearrange("p (c t) -> p c t", c=4)), [bPT], [bygT])
                    mm(PB[:], tm[:, j, 512:640], Xd[:], True, True, [btm, bXd], [bPB])
                    op("dve", lambda e: e.tensor_tensor(out=v3(H[:]), in0=v3(H[:]), in1=bc3(sm[:, 56:64]), op=ALU.mult), [bH, bsm], [bH])
                    op("dve", lambda e: e.tensor_tensor(out=H[:], in0=H[:], in1=PB[:], op=ALU.add), [bH, bPB], [bH])
                    op("act", lambda e: e.copy(out=Hbf[:], in_=H[:]), [bH], [bHbf])
                fw.dma("sp", lambda e: e.dma_start(out=XB[0:512, g * 512:(g + 1) * 512].rearrange("(c p) t -> p c t", p=128), in_=ygT[:]), [bygT], [bXBs])
            fw.barrier()

        def exchange(i):
            fw.dma("pool", lambda e: e.collective_compute("AllGather", ALU.bypass, replica_groups=[[0, 1, 2, 3], [4, 5, 6, 7]], ins=[XB[i * 64:(i + 1) * 64, :].rearrange("p (a b) -> (p a) b", b=1024)], outs=[GX[i * 256:(i + 1) * 256, :].rearrange("p (a b) -> (p a) b", b=1024)]), [bXBs if i < 8 else bXB], [bGX], inc=1)

        if stop > 1:
            for i in range(8):
                exchange(i)

        ckpt(1)
        with ExitStack() as st:
            w1b, bw1b = load_w_bf(st, "w1b", w1, 768, 1288)
            PA, bPA = PSB(st, "PA", [128, 512], F32)
            PS = [PSB(st, f"PS{i}", [128, 512], F32) for i in range(2)]
            PO = [PSB(st, f"PO{i}", [128, 512], F32) for i in range(4)]
            PT, bPT = PSB(st, "PT", [128, 1024], BF16)
            xs32, bxs32 = SB(st, "xs32", [128, 1024], F32); xsbf, bxsbf = SB(st, "xsbf", [128, 1024], BF16)
            xT, bxT = SB(st, "xT", [128, 8, 512], BF16)
            kT, _ = SB(st, "kT", [128, 2, 8192], BF16); bkT = [Buf() for _ in range(16)]
            Vall, _ = SB(st, "Vall", [128, 64, 2, 130], BF16); bV = [Buf() for _ in range(16)]
            bVone = Buf()
            op("pool", lambda e: e.memset(Vall[:, :, :, 128:129], 1.0), [], bV)
            qT, bqT = SB(st, "qT", [128, 2, 512], BF16)
            q32, bq32 = SB(st, "q32", [128, 512], F32)
            sq, bsq = SB(st, "sq", [128, 2, 512], BF16)
            nb, bnb = SB(st, "nb", [128, 24], F32)
            op("pool", lambda e: e.memset(nb[:], 0.0), [], [bnb])
            PTt = [SB(st, f"PTt{i}", [128, 512], BF16) for i in range(3)]
            o1, bo1 = SB(st, "o1", [128, 4, 128], F32)
            oo, boo = SB(st, "oo", [128, 4, 128], F32)
            junk, bjunk = SB(st, "junk", [128, 128], F32)
            on, bon = SB(st, "on", [128, 128], BF16)
            oT, boT = SB(st, "oT", [128, 2, 512], BF16)
            s2, bs2 = SB(st, "s2", [128, 16], F32)
            cnt = 0
            s32 = [SB(st, f"uv32_{i}", [128, 2048], F32) for i in range(3)]
            s16 = [SB(st, f"uv16_{i}", [128, 2048], BF16) for i in range(3)]

            def precast(i):
                a32, ba32 = s32[i % 3]
                a16, ba16 = s16[i % 3]
                fw.dma("pool", lambda e: e.dma_start(out=a32[:], in_=peer_uv[i * 128:(i + 1) * 128, :]), [], [ba32])
                op("dve", lambda e: e.tensor_copy(out=a16[:], in_=a32[:]), [ba32], [ba16])
                fw.dma("pool", lambda e: e.dma_start(out=UVB[i * 128:(i + 1) * 128, :], in_=a16[:]), [ba16], [bUVB])

            for g in range(16 if stop != 12 else 1):
                load_xT((xs32, bxs32, xsbf, bxsbf), xb, g * 512, xT, bxT, PT, bPT, cast_engs=("dve",))
                for i in range(8 * g, 8 * g + 8):
                    precast(i)
                cs = slice(g * 512, (g + 1) * 512)
                for hh in range(2):
                    for kc in range(8):
                        mm(PA[:], w1b[:, kc, 256 + hh * 128:256 + (hh + 1) * 128], xT[:, kc, :], kc == 0, kc == 7, [bw1b, bxT], [bPA])
                    op("dve", lambda e: e.tensor_copy(out=kT[:, hh, cs], in_=PA[:]), [bPA], [bkT[g]])
                    for kc in range(8):
                        mm(PA[:], w1b[:, kc, hh * 128:(hh + 1) * 128], xT[:, kc, :], kc == 0, kc == 7, [bw1b, bxT], [bPA])
                    op("dve", lambda e: e.tensor_copy(out=q32[:], in_=PA[:]), [bPA], [bq32])
                    op("dve", lambda e: e.tensor_scalar_mul(out=qT[:, hh, :], in0=q32[:], scalar1=0.125), [bq32], [bqT])
                    if DBG and g == 0 and hh == 0:
                        fw.dma("sp", lambda e: e.dma_start(out=d_pt, in_=qT[:, 0, :]), [bqT], [])
                for j in range(4):
                    for kc in range(8):
                        mm(PA[:, 0:256], xT[:, kc, j * 128:(j + 1) * 128], w1b[:, kc, 512:768], kc == 0, kc == 7, [bw1b, bxT], [bPA])
                    op("dve", lambda e: e.tensor_copy(out=Vall[:, g * 4 + j, :, 0:128], in_=PA[:, 0:256].rearrange("p (h d) -> p h d", h=2)), [bPA], [bV[g]])
                op("dve", lambda e: e.tensor_tensor(out=sq[:], in0=qT[:], in1=qT[:], op=ALU.mult), [bqT], [bsq])
                for hh in range(2):
                    for m in range(2):
                        mm(PA[:], sel_bf[:, m, :], sq[:, hh, :], True, True, [bsq] + C, [bPA])
                        op("dve", lambda e: e.reduce_max(out=nb[:, hh * 2 + m:hh * 2 + m + 1], in_=PA[:], axis=AX.X), [bPA], [bnb])
                op("dve", lambda e: e.tensor_tensor(out=sq[:], in0=kT[:, :, cs], in1=kT[:, :, cs], op=ALU.mult), [bkT[g]], [bsq])
                for hh in range(2):
                    for m in range(2):
                        mm(PA[:], sel_bf[:, m, :], sq[:, hh, :], True, True, [bsq] + C, [bPA])
                        op("dve", lambda e: e.reduce_max(out=nb[:, 4 + hh * 2 + m:5 + hh * 2 + m], in_=PA[:], axis=AX.X), [bPA], [bnb])
                op("dve", lambda e: e.tensor_tensor(out=nb[:, 8:12], in0=nb[:, 8:12], in1=nb[:, 4:8], op=ALU.max), [bnb], [bnb])
                op("dve", lambda e: e.tensor_tensor(out=nb[:, 12:16], in0=nb[:, 8:12], in1=nb[:, 0:4], op=ALU.mult), [bnb], [bnb])
                op("act", lambda e: e.activation(out=nb[:, 12:16], in_=nb[:, 12:16], func=AF.Sqrt), [bnb], [bnb])
                op("dve", lambda e: e.tensor_scalar_mul(out=nb[:, 16:20], in0=nb[:, 12:16], scalar1=-1.0), [bnb], [bnb])
                for hh in range(2):
                    for m in range(2):
                        ms = slice(m * 64, (m + 1) * 64)
                        nkb = 4 * g + 4

                        def emit_S(kb, u):
                            r = max(0, kb - 4 * g)
                            Ps, bPs = PS[u % 2]
                            mm(Ps[:, r * 128:512], kT[ms, hh, kb * 128:(kb + 1) * 128], qT[ms, hh, r * 128:512], True, True, [bkT[kb // 4], bqT], [bPs])

                        emit_S(0, cnt)
                        for kb in range(nkb):
                            r = max(0, kb - 4 * g)
                            Ps, bPs = PS[cnt % 2]
                            Pt, bPt = PTt[cnt % 3]
                            if kb + 1 < nkb:
                                emit_S(kb + 1, cnt + 1)
                            cnt += 1
                            op("act", lambda e: e.activation(out=Pt[:, r * 128:512], in_=Ps[:, r * 128:512], func=AF.Exp, bias=nb[:, 16 + hh * 2 + m:17 + hh * 2 + m]), [bPs, bnb], [bPt])
                            if kb >= 4 * g:
                                op("dve", lambda e: e.tensor_tensor(out=Pt[:, r * 128:(r + 1) * 128], in0=Pt[:, r * 128:(r + 1) * 128], in1=tri_bf[:], op=ALU.mult), [bPt] + C, [bPt])
                            for qb in range(r, 4):
                                mm(PO[qb][0][:, 0:129], Pt[:, qb * 128:(qb + 1) * 128], Vall[:, kb, hh, 0:129], kb == 0, kb == 4 * g + qb, [bPt, bV[kb // 4]], [PO[qb][1]])
                        for qb in range(4):
                            Pq, bPq = PO[qb]
                            op("dve", lambda e: e.reciprocal(out=s2[:, 0:1], in_=Pq[:, 128:129]), [bPq], [bs2])
                            if m == 0:
                                op("dve", lambda e: e.tensor_scalar_mul(out=o1[:, qb, :], in0=Pq[:, 0:128], scalar1=s2[:, 0:1]), [bPq, bs2], [bo1])
                            else:
                                op("dve", lambda e: e.tensor_tensor(out=s2[:, 1:2], in0=s2[:, 0:1], in1=neglam, op=ALU.mult), [bs2, b_sm], [bs2])
                                op("dve", lambda e: e.scalar_tensor_tensor(out=oo[:, qb, :], in0=Pq[:, 0:128], scalar=s2[:, 1:2], in1=o1[:, qb, :], op0=ALU.mult, op1=ALU.add), [bPq, bs2, bo1], [boo])
                    for qb in range(4):
                        op("dve", lambda e: e.memset(s2[:, 4:5], 0.0), [], [bs2])
                        op("act", lambda e: e.activation(out=junk[:], in_=oo[:, qb, :], func=AF.Square, accum_out=s2[:, 4:5]), [boo], [bjunk, bs2])
                        op("dve", lambda e: e.tensor_scalar(out=s2[:, 5:6], in0=s2[:, 4:5], scalar1=1.0 / 128.0, scalar2=RMS_EPS, op0=ALU.mult, op1=ALU.add), [bs2], [bs2])
                        op("act", lambda e: e.activation(out=s2[:, 6:7], in_=s2[:, 5:6], func=AF.Sqrt), [bs2], [bs2])
                        op("dve", lambda e: e.reciprocal(out=s2[:, 7:8], in_=s2[:, 6:7]), [bs2], [bs2])
                        op("dve", lambda e: e.scalar_tensor_tensor(out=on[:], in0=oo[:, qb, :], scalar=s2[:, 7:8], in1=subs[:], op0=ALU.mult, op1=ALU.mult), [boo, bs2] + C, [bon])
                        tr(PT[:, qb * 128:(qb + 1) * 128], on[:], ident_bf[:], [bon] + C, [bPT])
                    op("act", lambda e: e.copy(out=oT[:, hh, :], in_=PT[:, 0:512]), [bPT], [boT])
                    if DBG and g == 0 and hh == 0:
                        fw.dma("sp", lambda e: e.dma_start(out=d_att[:, 0:24], in_=nb[:]), [bnb], [])
                        fw.dma("sp", lambda e: e.dma_start(out=d_att[:, 24:40], in_=s2[:]), [bs2], [])
                        fw.dma("sp", lambda e: e.dma_start(out=d_att[:, 576:1088], in_=oo[:].rearrange("p a b -> p (a b)")), [boo], [])
                        fw.dma("sp", lambda e: e.dma_start(out=d_att[:, 1088:1344], in_=qT[:, 0, 0:128].bitcast(F32) if False else small[:, 0:1].broadcast_to([128, 256])), [b_sm], []) if False else None
                fw.dma("sp", lambda e: e.dma_start(out=XB[512:768, cs].rearrange("(h p) t -> p h t", p=128), in_=oT[:]), [boT], [bXB])
            fw.barrier()

        if DBG:
            fw.dma("sp", lambda e: e.dma_start(out=d_xb, in_=XB), [bXB, bXBs], [])
        ckpt(2)
        ckpt(12)
        for i in range(8, 12):
            exchange(i)
        GXV = GX.rearrange("r (q t) -> (r q) t", t=256)
        ckpt(3)

        with ExitStack() as stP2:
            st6, bst6 = SB(stP2, "st6", [128, 24], F32)
            xsbf, bxsbf = SB(stP2, "xsbf2", [128, 1024], BF16)
            xs32, bxs32 = SB(stP2, "xs32b", [128, 1024], F32)
            rres, brres = SB(stP2, "rres", [128, 1024], F32)
            lno, blno = SB(stP2, "lno", [128, 1024], F32)
            ltmp, bltmp = SB(stP2, "ltmp", [128, 1024], F32)
            idxT_t, bidxT = SB(stP2, "idxT_t", [128, 128], I32)
            gateT_t, bgateT = SB(stP2, "gateT_t", [128, 128], F32)
            bIDXD, bGATED = Buf(), Buf()
            print("sbuf remaining (P2 start):", nc.sbuf_bytes_remaining)

            def load_ln(li):
                fw.dma("sp", lambda e: e.dma_start(out=lnt[:], in_=lnp[:, 2 * li:2 * li + 2, :]), [], [blnt])

            def to_featT(src, bsrc, dstT, bdstT, col0, PT, bPT):
                cast(xsbf[:], src, [bsrc], [bxsbf])
                for kc in range(8):
                    tr(PT[:, kc * 128:(kc + 1) * 128], xsbf[:, kc * 128:(kc + 1) * 128], ident_bf[:], [bxsbf] + C, [bPT])
                op("act", lambda e: e.copy(out=dstT[:, :, col0:col0 + 128], in_=PT[:].rearrange("p (k t) -> p k t", k=8)), [bPT], [bdstT])

            with ExitStack() as stX:
                xfT, bxfT = SB(stX, "xfT", [128, 8, 2048], BF16)
                with ExitStack() as st2A:
                    mixT, bmixT = SB(st2A, "mixT", [128, 8, 2048], BF16)
                    with ExitStack() as st:
                        wsb, bwsb = load_w_bf(st, "wsb", w_ssd, 1024, 0, nk=16)
                        wdb, bwdb = load_w_bf(st, "wdb", w_diff, 1024, 0)
                        wgb, bwgb = load_w_bf(st, "wgb", wg, 2048, 0)
                        gbt, bgbt = SB(st, "gbt", [128, 16], F32)
                        fw.dma("sp", lambda e: e.dma_start(out=gbt[:], in_=gbias), [], [bgbt])
                        idt, bidt = SB(st, "idt", [128, 24, 8], I32)
                        fw.dma("sp", lambda e: e.dma_start(out=idt[:], in_=idq), [], [bidt])
                        PA, bPA = PSB(st, "PA", [128, 512], F32); PB, bPB = PSB(st, "PB", [128, 512], F32)
                        PY, bPY = PSB(st, "PY", [128, 512], F32); PYo, bPYo = PSB(st, "PYo", [128, 512], F32)
                        PT, bPT = PSB(st, "PT", [128, 1024], BF16)
                        GXt, bGXt = SB(st, "GXt", [128, 24, 256], BF16)
                        xT, bxT = SB(st, "xT", [128, 8, 256], BF16)
                        sa, bsa = SB(st, "sa", [128, 256], F32); sbb, bsbb = SB(st, "sbb", [128, 256], F32)
                        print("sbuf remaining (2A-1):", nc.sbuf_bytes_remaining)
                        for tt in range(8):
                            tcs = slice(tt * 256, (tt + 1) * 256)
                            for kb in range(24):
                                fw.dma("pool", lambda e: e.indirect_dma_start(out=GXt[:, kb, :], out_offset=None, in_=GXV, in_offset=bass.IndirectOffsetOnAxis(ap=idt[:, kb, tt:tt + 1], axis=0)), [bidt, bGX], [bGXt])
                            for j in range(2):
                                row0 = tt * 256 + j * 128
                                fw.dma("sp", lambda e: e.dma_start(out=xs32[:], in_=xown[row0:row0 + 128, :]), [], [bxs32])
                                to_featT(xs32[:], bxs32, xT, bxT, j * 128, PT, bPT)
                            for fb in range(8):
                                fs = slice(fb * 128, (fb + 1) * 128)
                                for kc in range(8):
                                    mm(PA[:, 0:256], wgb[:, kc, fs], xT[:, kc, :], kc == 0, kc == 7, [bwgb, bxT], [bPA])
                                for kc in range(8):
                                    mm(PB[:, 0:256], wgb[:, kc, 1024 + fb * 128:1024 + (fb + 1) * 128], xT[:, kc, :], kc == 0, kc == 7, [bwgb, bxT], [bPB])
                                i = 0
                                for r in range(4):
                                    for blk in range(4):
                                        mm(PY[:, 0:256], wsb[:, r * 4 + blk, fs], GXt[:, r * 6 + blk, :], i == 0, i == 15, [bwsb, bGXt], [bPY])
                                        i += 1
                                i = 0
                                for r in range(4):
                                    for hh in range(2):
                                        mm(PYo[:, 0:256], wdb[:, r * 2 + hh, fs], GXt[:, r * 6 + 4 + hh, :], i == 0, i == 7, [bwdb, bGXt], [bPYo])
                                        i += 1
                                op("act", lambda e: e.activation(out=sa[:], in_=PA[:, 0:256], func=AF.Sigmoid, bias=gbt[:, fb:fb + 1]), [bPA, bgbt], [bsa])
                                op("act", lambda e: e.activation(out=sbb[:], in_=PB[:, 0:256], func=AF.Sigmoid, bias=gbt[:, 8 + fb:9 + fb]), [bPB, bgbt], [bsbb])
                                op("dve", lambda e: e.tensor_tensor(out=sa[:], in0=sa[:], in1=PY[:, 0:256], op=ALU.mult), [bsa, bPY], [bsa])
                                op("dve", lambda e: e.tensor_tensor(out=sbb[:], in0=sbb[:], in1=PYo[:, 0:256], op=ALU.mult), [bsbb, bPYo], [bsbb])
                                op("pool", lambda e: e.tensor_tensor(out=mixT[:, fb, tcs], in0=sa[:], in1=sbb[:], op=ALU.add), [bsa, bsbb], [bmixT])
                        fw.barrier()
                    ckpt(4)
                    with ExitStack() as st:
                        wob, bwob = load_w_bf(st, "wob", w_o, 1024, 0)
                        load_ln(0)
                        PS0, bPS0 = PSB(st, "PS0", [128, 512], F32); PS1, bPS1 = PSB(st, "PS1", [128, 512], F32)
                        PT, bPT = PSB(st, "PT", [128, 1024], BF16)
                        for tk in range(16):
                            row0 = tk * 128
                            fw.dma("sp", lambda e: e.dma_start(out=xs32[:], in_=xown[row0:row0 + 128, :]), [], [bxs32])
                            for half, (Ph, bPh) in enumerate(((PS0, bPS0), (PS1, bPS1))):
                                for fb in range(8):
                                    mm(Ph[:], mixT[:, fb, row0:row0 + 128], wob[:, fb, half * 512:(half + 1) * 512], fb == 0, fb == 7, [bmixT, bwob], [bPh])
                                op("dve", lambda e: e.scalar_tensor_tensor(out=rres[:, half * 512:(half + 1) * 512], in0=xs32[:, half * 512:(half + 1) * 512], scalar=ALPHA, in1=Ph[:], op0=ALU.mult, op1=ALU.add), [bxs32, bPh], [brres])
                            layer_norm(rres, brres, 0, lno, blno, ltmp, bltmp, st6, bst6)
                            fw.dma("sp", lambda e: e.dma_start(out=X1D[row0:row0 + 128, :], in_=lno[:]), [blno], [bX1D])
                            if DBG:
                                fw.dma("sp", lambda e: e.dma_start(out=d_x1[row0:row0 + 128, :], in_=lno[:]), [blno], [])
                            to_featT(lno[:], blno, xfT, bxfT, row0, PT, bPT)
                        fw.barrier()

                ckpt(5)
                with ExitStack() as st:
                    wcq, bwcq = load_w_bf(st, "wcq", w_cq, 1024)
                    wck, bwck = load_w_bf(st, "wck", w_ck, 1024)
                    wcv, bwcv = load_w_bf(st, "wcv", w_cv, 1024)
                    wco, bwco = load_w_bf(st, "wco", w_co, 1024)
                    load_ln(1)
                    PA, bPA = PSB(st, "PA", [128, 512], F32); PB, bPB = PSB(st, "PB", [128, 512], F32)
                    PS0, bPS0 = PSB(st, "PS0", [128, 512], F32); PS1, bPS1 = PSB(st, "PS1", [128, 512], F32)
                    PO_, bPO_ = PSB(st, "PO", [128, 512], F32); PZ, bPZ = PSB(st, "PZ", [128, 512], F32)
                    PT, bPT = PSB(st, "PT", [128, 1024], BF16)
                    memT, bmemT = SB(st, "memT", [128, 8, 256], BF16)
                    kcT, bkcT = SB(st, "kcT", [128, 8, 256], BF16)
                    vc, bvc = SB(st, "vc", [128, 2, 1024], BF16)
                    qcT, bqcT = SB(st, "qcT", [128, 8, 256], BF16)
                    sqc, bsqc = SB(st, "sqc", [128, 2, 256], BF16)
                    nbc, bnbc = SB(st, "nbc", [128, 16], F32)
                    Pc, bPc = SB(st, "Pc", [128, 2, 256], BF16)
                    rz, brz = SB(st, "rz", [128, 256], F32)
                    ocT, bocT = SB(st, "ocT", [128, 8, 256], BF16)
                    print("sbuf remaining (2B):", nc.sbuf_bytes_remaining)
                    for mb in range(2):
                        fw.dma("sp", lambda e: e.dma_start(out=xs32[:], in_=memb[mb * 128:(mb + 1) * 128, :]), [], [bxs32])
                        to_featT(xs32[:], bxs32, memT, bmemT, mb * 128, PT, bPT)
                    for fb in range(8):
                        for kc in range(8):
                            mm(PA[:, 0:256], wck[:, kc, fb * 128:(fb + 1) * 128], memT[:, kc, :], kc == 0, kc == 7, [bwck, bmemT], [bPA])
                        op("dve", lambda e: e.tensor_copy(out=kcT[:, fb, :], in_=PA[:, 0:256]), [bPA], [bkcT])
                    for mb in range(2):
                        for half in range(2):
                            for kc in range(8):
                                mm(PA[:], memT[:, kc, mb * 128:(mb + 1) * 128], wcv[:, kc, half * 512:(half + 1) * 512], kc == 0, kc == 7, [bwcv, bmemT], [bPA])
                            op("dve", lambda e: e.tensor_copy(out=vc[:, mb, half * 512:(half + 1) * 512], in_=PA[:]), [bPA], [bvc])
                    for h in range(4):
                        op("dve", lambda e: e.tensor_tensor(out=sqc[:], in0=kcT[:, 2 * h:2 * h + 2, :], in1=kcT[:, 2 * h:2 * h + 2, :], op=ALU.mult), [bkcT], [bsqc])
                        for ch in range(2):
                            mm(PA[:, 0:256], ones_bf[:], sqc[:, ch, :], ch == 0, ch == 1, [bsqc] + C, [bPA])
                        op("dve", lambda e: e.reduce_max(out=nbc[:, h:h + 1], in_=PA[:, 0:256], axis=AX.X), [bPA], [bnbc])
                    for tt in range(8):
                        tcs = slice(tt * 256, (tt + 1) * 256)
                        for fb in range(8):
                            for kc in range(8):
                                mm(PA[:, 0:256], wcq[:, kc, fb * 128:(fb + 1) * 128], xfT[:, kc, tcs], kc == 0, kc == 7, [bwcq, bxfT], [bPA])
                            op("dve", lambda e: e.tensor_scalar_mul(out=qcT[:, fb, :], in0=PA[:, 0:256], scalar1=1.0 / 16.0), [bPA], [bqcT])
                        for h in range(4):
                            op("dve", lambda e: e.tensor_tensor(out=sqc[:], in0=qcT[:, 2 * h:2 * h + 2, :], in1=qcT[:, 2 * h:2 * h + 2, :], op=ALU.mult), [bqcT], [bsqc])
                            for ch in range(2):
                                mm(PB[:, 0:256], ones_bf[:], sqc[:, ch, :], ch == 0, ch == 1, [bsqc] + C, [bPB])
                            op("dve", lambda e: e.reduce_max(out=nbc[:, 4:5], in_=PB[:, 0:256], axis=AX.X), [bPB], [bnbc])
                            op("dve", lambda e: e.tensor_tensor(out=nbc[:, 5:6], in0=nbc[:, 4:5], in1=nbc[:, h:h + 1], op=ALU.mult), [bnbc], [bnbc])
                            op("act", lambda e: e.activation(out=nbc[:, 6:7], in_=nbc[:, 5:6], func=AF.Sqrt), [bnbc], [bnbc])
                            op("dve", lambda e: e.tensor_scalar_mul(out=nbc[:, 7:8], in0=nbc[:, 6:7], scalar1=-1.0), [bnbc], [bnbc])
                            for mb, (Ph, bPh) in enumerate(((PS0, bPS0), (PS1, bPS1))):
                                for ch in range(2):
                                    mm(Ph[:, 0:256], kcT[:, 2 * h + ch, mb * 128:(mb + 1) * 128], qcT[:, 2 * h + ch, :], ch == 0, ch == 1, [bkcT, bqcT], [bPh])
                                op("act", lambda e: e.activation(out=Pc[:, mb, :], in_=Ph[:, 0:256], func=AF.Exp, bias=nbc[:, 7:8]), [bPh, bnbc], [bPc])
                            for mb in range(2):
                                mm(PZ[:, 0:256], ones_bf[:], Pc[:, mb, :], mb == 0, mb == 1, [bPc] + C, [bPZ])
                            op("dve", lambda e: e.reciprocal(out=rz[:], in_=PZ[:, 0:256]), [bPZ], [brz])
                            for ch in range(2):
                                for mb in range(2):
                                    mm(PO_[:, 0:256], vc[:, mb, h * 256 + ch * 128:h * 256 + (ch + 1) * 128], Pc[:, mb, :], mb == 0, mb == 1, [bvc, bPc], [bPO_])
                                op("dve", lambda e: e.tensor_tensor(out=ocT[:, 2 * h + ch, :], in0=PO_[:, 0:256], in1=rz[:], op=ALU.mult), [bPO_, brz], [bocT])
                        for j in range(2):
                            row0 = tt * 256 + j * 128
                            fw.dma("sp", lambda e: e.dma_start(out=xs32[:], in_=X1D[row0:row0 + 128, :]), [bX1D], [bxs32])
                            for half, (Ph, bPh) in enumerate(((PS0, bPS0), (PS1, bPS1))):
                                for fb in range(8):
                                    mm(Ph[:], ocT[:, fb, j * 128:(j + 1) * 128], wco[:, fb, half * 512:(half + 1) * 512], fb == 0, fb == 7, [bocT, bwco], [bPh])
                                op("dve", lambda e: e.scalar_tensor_tensor(out=rres[:, half * 512:(half + 1) * 512], in0=xs32[:, half * 512:(half + 1) * 512], scalar=ALPHA, in1=Ph[:], op0=ALU.mult, op1=ALU.add), [bxs32, bPh], [brres])
                            layer_norm(rres, brres, 1, lno, blno, ltmp, bltmp, st6, bst6)
                            fw.dma("sp", lambda e: e.dma_start(out=X2D[row0:row0 + 128, :], in_=lno[:]), [blno], [bX2D])
                            if DBG:
                                fw.dma("sp", lambda e: e.dma_start(out=d_x2[row0:row0 + 128, :], in_=lno[:]), [blno], [])
                            to_featT(lno[:], blno, xfT, bxfT, row0, PT, bPT)
                    fw.barrier()

                ckpt(6)
                with ExitStack() as st:
                    wpq, bwpq = load_w_bf(st, "wpq", w_pq, 2048)
                    PA, bPA = PSB(st, "PA", [128, 512], F32)
                    PT, bPT = PSB(st, "PT", [128, 1024], BF16)
                    skT, bskT = SB(st, "skT", [128, 16, 128], BF16)
                    kbf, bkbf = SB(st, "kbf", [128, 128], BF16)
                    for hh in range(16):
                        fw.dma("sp", lambda e: e.dma_start(out=xs32[:, 0:128], in_=subk[hh]), [], [bxs32])
                        cast(kbf[:], xs32[:, 0:128], [bxs32], [bkbf])
                        tr(PT[:, 0:128], kbf[:], ident_bf[:], [bkbf] + C, [bPT])
                        op("act", lambda e: e.copy(out=skT[:, hh, :], in_=PT[:, 0:128]), [bPT], [bskT])
                    qpT, bqpT = SB(st, "qpT", [128, 16, 128], BF16)
                    S, bS = SB(st, "S", [128, 16, 128], F32)
                    wk, bwk = SB(st, "wk", [128, 256], F32)
                    m16, bm16 = SB(st, "m16", [128, 16, 16], F32)
                    i16, bi16 = SB(st, "i16", [128, 16, 16], U32)
                    i16f, bi16f = SB(st, "i16f", [128, 16, 16], F32)
                    cs_, bcs = SB(st, "cands", [128, 8, 256], F32)
                    ci_, bci = SB(st, "candi", [128, 8, 256], F32)
                    best, bbest = SB(st, "best", [128, 8, 16], F32)
                    idxf, bidxf = SB(st, "idxf", [128, 128], F32)
                    gate, bgate = SB(st, "gate", [128, 8, 16], F32)
                    g8, bg8 = SB(st, "g8", [128, 16], F32)
                    c3 = lambda ap: ap.rearrange("p (a b) -> p a b", a=16)
                    bm16h = [Buf() for _ in range(16)]
                    bi16h = [Buf() for _ in range(16)]
                    bcsh = [Buf() for _ in range(8)]
                    bcih = [Buf() for _ in range(8)]
                    bbesth = [Buf() for _ in range(8)]
                    bidxc = [Buf() for _ in range(128)]
                    wks = [SB(st, f"wk{i}", [128, 256], F32) for i in range(4)]
                    nw = 0
                    for tk in range(16):
                        tcs = slice(tk * 128, (tk + 1) * 128)
                        for q4 in range(4):
                            for hq in range(4):
                                hh = q4 * 4 + hq
                                for kc in range(8):
                                    mm(PA[:, hq * 128:(hq + 1) * 128], wpq[:, kc, hh * 128:(hh + 1) * 128], xfT[:, kc, tcs], kc == 0, kc == 7, [bwpq, bxfT], [bPA])
                            op("act", lambda e: e.copy(out=qpT[:, q4 * 4:(q4 + 1) * 4, :], in_=PA[:].rearrange("p (a b) -> p a b", a=4)), [bPA], [bqpT])
                        for q4 in range(4):
                            for hq in range(4):
                                hh = q4 * 4 + hq
                                mm(PA[:, hq * 128:(hq + 1) * 128], qpT[:, hh, :], skT[:, hh, :], True, True, [bqpT, bskT], [bPA])
                            op("act", lambda e: e.copy(out=S[:, q4 * 4:(q4 + 1) * 4, :], in_=PA[:].rearrange("p (a b) -> p a b", a=4)), [bPA], [bS])
                        for hh in range(16):
                            wk_, bwk_ = wks[nw % 4]; nw += 1
                            op("dve", lambda e: e.max(out=m16[:, hh, 0:8], in_=S[:, hh, :]), [bS], [bm16h[hh]])
                            op("dve", lambda e: e.match_replace(out=wk_[:, 0:128], in_to_replace=m16[:, hh, 0:8], in_values=S[:, hh, :], imm_value=-1e30), [bS, bm16h[hh]], [bwk_])
                            op("dve", lambda e: e.max(out=m16[:, hh, 8:16], in_=wk_[:, 0:128]), [bwk_], [bm16h[hh]])
                            op("dve", lambda e: e.max_index(out=i16[:, hh, 0:8], in_max=m16[:, hh, 0:8], in_values=S[:, hh, :]), [bS, bm16h[hh]], [bi16h[hh]])
                            op("dve", lambda e: e.max_index(out=i16[:, hh, 8:16], in_max=m16[:, hh, 8:16], in_values=S[:, hh, :]), [bS, bm16h[hh]], [bi16h[hh]])
                        op("dve", lambda e: e.tensor_copy(out=i16f[:], in_=i16[:]), bi16h, [bi16f])
                        for h in range(8):
                            op("pool", lambda e: e.tensor_tensor(out=c3(cs_[:, h, :]), in0=m16[:, 2 * h, :].unsqueeze(2).broadcast_to([128, 16, 16]), in1=m16[:, 2 * h + 1, :].unsqueeze(1).broadcast_to([128, 16, 16]), op=ALU.add), [bm16h[2 * h], bm16h[2 * h + 1]], [bcsh[h]])
                            op("dve", lambda e: e.scalar_tensor_tensor(out=c3(ci_[:, h, :]), in0=i16f[:, 2 * h, :].unsqueeze(2).broadcast_to([128, 16, 16]), scalar=128.0, in1=i16f[:, 2 * h + 1, :].unsqueeze(1).broadcast_to([128, 16, 16]), op0=ALU.mult, op1=ALU.add), [bi16f], [bcih[h]])
                        for h in range(8):
                            wk_, bwk_ = wks[nw % 4]; nw += 1
                            op("dve", lambda e: e.max(out=best[:, h, 0:8], in_=cs_[:, h, :]), [bcsh[h]], [bbesth[h]])
                            op("dve", lambda e: e.match_replace(out=wk_[:], in_to_replace=best[:, h, 0:8], in_values=cs_[:, h, :], imm_value=-1e30), [bcsh[h], bbesth[h]], [bwk_])
                            op("dve", lambda e: e.max(out=best[:, h, 8:16], in_=wk_[:]), [bwk_], [bbesth[h]])
                        op("pool", lambda e: e.memset(idxf[:], 0.0), [], bidxc + [bidxf])
                        for h in range(8):
                            for k in range(16):
                                op("dve", lambda e: e.scalar_tensor_tensor(out=wk[:], in0=cs_[:, h, :], scalar=best[:, h, k:k + 1], in1=ci_[:, h, :], op0=ALU.is_equal, op1=ALU.mult, accum_out=idxf[:, h * 16 + k:h * 16 + k + 1]), [bcsh[h], bcih[h], bbesth[h]], [bidxc[h * 16 + k]])
                        bbest_all = bbesth
                        op("dve", lambda e: e.tensor_scalar_min(out=idxf[:], in0=idxf[:], scalar1=16383.0), bidxc + [bidxf], [bidxf])
                        op("dve", lambda e: e.tensor_tensor(out=gate[:], in0=best[:], in1=best[:, :, 0:1].broadcast_to([128, 8, 16]), op=ALU.subtract), bbesth, [bgate])
                        op("act", lambda e: e.activation(out=gate[:], in_=gate[:], func=AF.Exp), [bgate], [bgate])
                        op("dve", lambda e: e.reduce_sum(out=g8[:, 0:8], in_=gate[:], axis=AX.X), [bgate], [bg8])
                        op("dve", lambda e: e.reciprocal(out=g8[:, 8:16], in_=g8[:, 0:8]), [bg8], [bg8])
                        op("dve", lambda e: e.tensor_tensor(out=gate[:], in0=gate[:], in1=g8[:, 8:16].unsqueeze(2).broadcast_to([128, 8, 16]), op=ALU.mult), [bgate, bg8], [bgate])
                        op("pe", lambda e: e.transpose(out=PA[:, 0:128], in_=idxf[:], identity=ident_f[:]), [bidxf] + C, [bPA], skip_self=True)
                        op("pe", lambda e: e.transpose(out=PA[:, 128:256], in_=gate[:].rearrange("p a b -> p (a b)"), identity=ident_f[:]), [bgate] + C, [bPA], skip_self=True)
                        op("dve", lambda e: e.tensor_copy(out=idxT_t[:], in_=PA[:, 0:128]), [bPA], [bidxT])
                        op("dve", lambda e: e.tensor_copy(out=gateT_t[:], in_=PA[:, 128:256]), [bPA], [bgateT])
                        fw.dma("sp", lambda e: e.dma_start(out=IDXD[:, tcs], in_=idxT_t[:]), [bidxT], [bIDXD])
                        fw.dma("sp", lambda e: e.dma_start(out=GATED[:, tcs], in_=gateT_t[:]), [bgateT], [bGATED])
                    fw.barrier()

            ckpt(7)
            with ExitStack() as st:
                load_ln(2)
                PA, bPA = PSB(st, "PA", [128, 512], F32)
                PXB = [PSB(st, f"PXB{i}", [128, 1024], F32) for i in range(2)]
                PYT, bPYT = PSB(st, "PYT", [128, 1024], F32)
                RS, bRS = SB(st, "RS", [128, 128, 128], BF16)
                op("pool", lambda e: e.affine_select(out=RS[:], in_=ones_f[:].unsqueeze(1).broadcast_to([128, 128, 128]), pattern=[[-1, 128], [0, 128]], compare_op=ALU.is_equal, fill=0.0, base=0, channel_multiplier=1), C, [bRS])
                hidT, _ = SB(st, "hidT", [128, 128], F32)
                gcol, _ = SB(st, "gcol", [128, 128], F32)
                Rc, bRc = SB(st, "Rc", [128, 255], BF16)
                op("pool", lambda e: e.memset(Rc[:], 0.0), [], [bRc])
                op("pool", lambda e: e.memset(Rc[:, 127:128], 1.0), [bRc], [bRc])
                Lt = [SB(st, f"Lt{i}", [128, 128], BF16) for i in range(4)]
                bhid = [Buf() for _ in range(128)]
                bgc = [Buf() for _ in range(128)]
                bwc = [Buf() for _ in range(128)]
                x2b, bx2b = SB(st, "x2b", [128, 1024], BF16)
                NG = 16
                UVg = [SB(st, f"UVg{i}", [128, 2048], BF16) for i in range(NG)]
                junk2, bjunk2 = SB(st, "junk2", [128, 1024], F32)
                yT, byT = SB(st, "yT", [128, 8, 128], F32)
                print("sbuf remaining (2C-2):", nc.sbuf_bytes_remaining)
                for tk in range(16):
                    fw.dma("sp", lambda e: e.dma_start(out=xs32[:], in_=X2D[tk * 128:(tk + 1) * 128, :]), [bX2D], [bxs32])
                    cast(x2b[:], xs32[:], [bxs32], [bx2b])
                    fw.dma("sp", lambda e: e.dma_start(out=idxT_t[:], in_=IDXD[:, tk * 128:(tk + 1) * 128]), [bIDXD], [bidxT])
                    fw.dma("sp", lambda e: e.dma_start(out=gateT_t[:], in_=GATED[:, tk * 128:(tk + 1) * 128]), [bGATED], [bgateT])
                    op("dve", lambda e: e.memset(hidT[:], 0.0), [], bhid)
                    for tt_ in range(128 + 2):
                        if tt_ < 128:
                            t = tt_
                            UV_, bUV_ = UVg[t % NG]
                            Px, bPx = PXB[t % 2]
                            fw.dma("pool", lambda e: e.indirect_dma_start(out=UV_[:], out_offset=None, in_=UVB, in_offset=bass.IndirectOffsetOnAxis(ap=idxT_t[:, t:t + 1], axis=0)), [bidxT, bUVB], [bUV_], nslots=16)
                            for half in range(2):
                                mm(Px[:, half * 512:(half + 1) * 512], RS[:, t, :], x2b[:, half * 512:(half + 1) * 512], True, True, [bRS, bx2b], [bPx])
                            op("dve", lambda e: e.scalar_tensor_tensor(out=junk2[:], in0=UV_[:, 0:1024], scalar=1.0, in1=Px[:], op0=ALU.mult, op1=ALU.mult, accum_out=hidT[:, t:t + 1]), [bUV_, bPx], [bjunk2, bhid[t]])
                            op("act", lambda e: e.activation(out=gcol[:, t:t + 1], in_=hidT[:, t:t + 1], func=AF.Gelu), [bhid[t]], [bgc[t]])
                        if 0 <= tt_ - 1 < 128:
                            t = tt_ - 1
                            L_, bL_ = Lt[t % 4]
                            op("dve", lambda e: e.tensor_scalar(out=L_[:], in0=Rc[:, 127 - t:255 - t], scalar1=gcol[:, t:t + 1], scalar2=gateT_t[:, t:t + 1], op0=ALU.mult, op1=ALU.mult), [bRc, bgc[t], bgateT], [bL_])
                        if 0 <= tt_ - 2 < 128:
                            t = tt_ - 2
                            UV_, bUV_ = UVg[t % NG]
                            L_, bL_ = Lt[t % 4]
                            for half in range(2):
                                mm(PYT[:, half * 512:(half + 1) * 512], L_[:], UV_[:, 1024 + half * 512:1024 + (half + 1) * 512], t == 0, t == 127, [bUV_, bL_], [bPYT])
                    op("dve", lambda e: e.scalar_tensor_tensor(out=rres[:], in0=xs32[:], scalar=ALPHA, in1=PYT[:], op0=ALU.mult, op1=ALU.add), [bxs32, bPYT], [brres])
                    layer_norm(rres, brres, 2, lno, blno, ltmp, bltmp, st6, bst6)
                    fw.dma("sp", lambda e: e.dma_start(out=out[tk * 128:(tk + 1) * 128, :], in_=lno[:]), [blno], [])
                fw.barrier()
    return nc


_NC = {}


def _prep(inputs):
    x = np.asarray(inputs["x"], np.float32); mem = np.asarray(inputs["mem"], np.float32)
    w_in = np.asarray(inputs["w_in"][0], np.float32)
    conv_w = np.asarray(inputs["conv_w"][0], np.float32); conv_b = np.asarray(inputs["conv_b"][0], np.float32)
    rep = lambda v: np.ascontiguousarray(np.broadcast_to(np.asarray(v, np.float32).reshape(1, -1), (128, np.asarray(v).size)))
    lnp = np.stack([rep(inputs[k][0]) for k in ("ln1_g", "ln1_b", "ln2_g", "ln2_b", "ln3_g", "ln3_b")], axis=1)
    gb = np.asarray(inputs["gate_bias"][0], np.float32).reshape(16, 128).T
    shared = dict(
        wg=np.ascontiguousarray(w_in[:, 8224:10272]), lnp=np.ascontiguousarray(lnp), gbias=np.ascontiguousarray(gb),
        w_ssd_br=np.asarray(inputs["w_ssd_br"][0], np.float32), w_diff_br=np.asarray(inputs["w_diff_br"][0], np.float32),
        w_o=np.asarray(inputs["w_o"][0], np.float32), w_cq=np.asarray(inputs["w_cq"][0], np.float32),
        w_ck=np.asarray(inputs["w_ck"][0], np.float32), w_cv=np.asarray(inputs["w_cv"][0], np.float32),
        w_co=np.asarray(inputs["w_co"][0], np.float32), w_pq=np.asarray(inputs["w_pq"][0], np.float32),
        sub_keys=np.ascontiguousarray(np.asarray(inputs["sub_keys"][0], np.float32).reshape(16, 128, 128)),
        peer_uv=np.ascontiguousarray(np.concatenate([np.asarray(inputs["peer_u"][0], np.float32), np.asarray(inputs["peer_v"][0], np.float32)], axis=1)),
    )
    maps = []
    for core in range(8):
        b, c = core // 4, core % 4
        cols = np.concatenate([
            2048 + c * 512 + np.arange(512),
            2048 + 2048 + c * 128 + np.arange(128),
            2048 + 2048 + 512 + c * 128 + np.arange(128),
            c * 512 + np.arange(512),
            5120 + c * 8 + np.arange(8),
            5152 + c * 256 + np.arange(256),
            6176 + c * 256 + np.arange(256),
            7200 + c * 256 + np.arange(256),
        ])
        ccols = np.concatenate([c * 512 + np.arange(512), 2048 + c * 128 + np.arange(128), 2048 + 512 + c * 128 + np.arange(128)])
        cwc = conv_w[:, ccols].T.reshape(6, 128, 4).transpose(1, 0, 2)
        cbc = conv_b[ccols].reshape(6, 128).T
        hs = slice(c * 8, (c + 1) * 8)
        rowp = np.concatenate([
            rep(inputs["dt_bias"][0][hs]), rep(inputs["a_log"][0][hs]), rep(inputs["d_skip"][0][hs]),
            rep(inputs["ssd_norm_w"][0][c * 512:(c + 1) * 512]), rep(inputs["subln_w"][0]),
            rep(np.asarray(inputs["lam_q"][0]).reshape(-1)), rep(np.asarray(inputs["lam_k"][0]).reshape(-1))], axis=1)
        assert rowp.shape == (128, NR)
        p = np.arange(128)[:, None, None]; kb = np.arange(24)[None, :, None]; tt = np.arange(8)[None, None, :]
        row = (kb % 6) * 128 + p
        R = (row // 64) * 256 + (kb // 6) * 64 + (row % 64)
        idq = ((R * 4 + c) * 8 + tt).astype(np.int32)
        m = dict(shared)
        m.update(xb=np.ascontiguousarray(x[b]), xown=np.ascontiguousarray(x[b, c * 2048:(c + 1) * 2048]),
                 memb=np.ascontiguousarray(mem[b]), w1=np.ascontiguousarray(w_in[:, cols]),
                 cw=np.ascontiguousarray(cwc), cbv=np.ascontiguousarray(cbc), rowp=np.ascontiguousarray(rowp),
                 idq=np.ascontiguousarray(idq))
        maps.append(m)
    return maps


def run(inputs, DBG=False, stop=99):
    key = (bool(DBG), stop)
    if key not in _NC:
        _NC[key] = build(DBG, stop)
    maps = _prep(inputs)
    res = run_bass_kernel_spmd(_NC[key], maps, core_ids=list(range(8)))
    return res.results


def kernel(**inputs):
    R = run(inputs, False)
    out = np.empty((2, 8192, 1024), np.float32)
    for core in range(8):
        b, c = core // 4, core % 4
        out[b, c * 2048:(c + 1) * 2048] = R[core]["out"]
    return out
```

```python
from contextlib import ExitStack
import numpy as np
import concourse.bass as bass
import concourse.mybir as mybir
from concourse.bass_utils import run_bass_kernel_spmd

F32 = mybir.dt.float32
BF16 = mybir.dt.bfloat16
I32 = mybir.dt.int32
U32 = mybir.dt.uint32
AF = mybir.ActivationFunctionType
ALU = mybir.AluOpType
AX = mybir.AxisListType

ALPHA = 2.0 ** 0.25
LN_EPS = 1e-5
RMS_EPS = 1e-5
LAMBDA_INIT = 0.8 - 0.6
NR = 920
NEG = -30000.0


class Buf:
    __slots__ = ("w", "r")

    def __init__(self):
        self.w = None
        self.r = {}


class FW:
    EPOCH = 12000

    def __init__(self, nc, stack):
        self.nc = nc
        self.stack = stack
        self.nsem = 0
        self.eng = {}
        for name, e in (("pe", nc.tensor), ("act", nc.scalar), ("dve", nc.vector),
                        ("pool", nc.gpsimd), ("sp", nc.sync)):
            self.eng[name] = dict(e=e, sem=None, cnt=0, seen={})
            self._new_sem(name)
        self.dq = {}

    def _mk_sem(self, tag):
        self.nsem += 1
        return self.stack.enter_context(self.nc.semaphore(f"{tag}_{self.nsem}"))

    def _new_sem(self, name):
        E = self.eng[name]
        E["sem"] = self._mk_sem("s" + name)
        E["cnt"] = 0

    def _wait(self, E, tok):
        sem, val = tok
        k = id(sem)
        if E["seen"].get(k, 0) >= val:
            return
        E["e"].wait_ge(sem, val)
        E["seen"][k] = val

    def _deps(self, E, reads, writes, skip_self=False):
        toks = {}

        def add(t):
            if t is None:
                return
            k = id(t[0])
            if k not in toks or toks[k][1] < t[1]:
                toks[k] = t
        for b in reads:
            add(b.w)
        for b in writes:
            add(b.w)
            for t in b.r.values():
                add(t)
        for t in toks.values():
            if skip_self and t[0] is E["sem"]:
                continue
            self._wait(E, t)

    def _mark(self, tok, reads, writes):
        k = id(tok[0])
        for b in reads:
            b.r[k] = tok
        for b in writes:
            b.w = tok
            b.r = {}

    def op(self, name, fn, reads=(), writes=(), skip_self=False):
        E = self.eng[name]
        if E["cnt"] >= self.EPOCH:
            self._new_sem(name)
        self._deps(E, reads, writes, skip_self)
        ins = fn(E["e"])
        E["cnt"] += 1
        ins.then_inc(E["sem"], 1)
        tok = (E["sem"], E["cnt"])
        self._mark(tok, reads, writes)
        return tok

    def dma(self, q, fn, reads=(), writes=(), nslots=8, inc=16, slotq=None):
        E = self.eng[q]
        D = self.dq.setdefault(slotq or q, dict(slots=[], i=0, eng=q))
        if len(D["slots"]) < nslots:
            D["slots"].append(dict(sem=self._mk_sem("d" + q), val=0))
            s = D["slots"][-1]
        else:
            s = D["slots"][D["i"] % nslots]
        D["i"] += 1
        if s["val"] >= 16 * 900:
            self._wait(E, (s["sem"], s["val"]))
            s["sem"] = self._mk_sem("d" + q)
            s["val"] = 0
        if s["val"] > 0:
            self._wait(E, (s["sem"], s["val"]))
        self._deps(E, reads, writes)
        ins = fn(E["e"])
        s["val"] += inc
        ins.then_inc(s["sem"], inc)
        tok = (s["sem"], s["val"])
        self._mark(tok, reads, writes)
        return tok

    def barrier(self):
        toks = []
        for E in self.eng.values():
            if E["cnt"] > 0:
                toks.append((E["sem"], E["cnt"]))
        for D in self.dq.values():
            for s in D["slots"]:
                if s["val"] > 0:
                    toks.append((s["sem"], s["val"]))
        for E in self.eng.values():
            for t in toks:
                if t[0] is E["sem"]:
                    continue
                self._wait(E, t)

    def drain(self, q):
        E = self.eng[q]
        for s in self.dq.get(q, dict(slots=[]))["slots"]:
            if s["val"] > 0:
                self._wait(E, (s["sem"], s["val"]))


class _Stop(Exception):
    pass


def build(DBG=False, stop=99):
    try:
        return _build(DBG, stop)
    except _Stop as s:
        return s.args[0]


def _build(DBG=False, stop=99):
    nc = bass.Bass("TRN2", target_bir_lowering=False)

    def din(name, shape, dt=F32):
        return nc.dram_tensor(name, shape, dt, kind="ExternalInput").ap()

    xb = din("xb", [8192, 1024]); xown = din("xown", [2048, 1024]); memb = din("memb", [256, 1024])
    w1 = din("w1", [1024, 2056]); wg = din("wg", [1024, 2048])
    cw = din("cw", [128, 6, 4]); cbv = din("cbv", [128, 6]); rowp = din("rowp", [128, NR])
    lnp = din("lnp", [128, 6, 1024]); gbias = din("gbias", [128, 16])
    w_ssd = din("w_ssd_br", [2048, 1024]); w_diff = din("w_diff_br", [1024, 1024]); w_o = din("w_o", [1024, 1024])
    w_cq = din("w_cq", [1024, 1024]); w_ck = din("w_ck", [1024, 1024]); w_cv = din("w_cv", [1024, 1024]); w_co = din("w_co", [1024, 1024])
    w_pq = din("w_pq", [1024, 2048]); subk = din("sub_keys", [16, 128, 128])
    peer_uv = din("peer_uv", [16384, 2048])
    idq = din("idq", [128, 24, 8], I32)
    out = nc.dram_tensor("out", [2048, 1024], F32, kind="ExternalOutput").ap()
    XB = nc.dram_tensor("xbuf", [768, 8192], BF16).ap()
    GX = nc.dram_tensor("gxbuf", [3072, 8192], BF16).ap()
    X1D = nc.dram_tensor("x1d", [2048, 1024], F32).ap()
    X2D = nc.dram_tensor("x2d", [2048, 1024], F32).ap()
    IDXD = nc.dram_tensor("idxd", [128, 2048], I32).ap()
    UVB = nc.dram_tensor("uvb", [16384, 2048], BF16).ap()
    GATED = nc.dram_tensor("gated", [128, 2048], F32).ap()
    if DBG:
        d_xb = nc.dram_tensor("d_xb", [768, 8192], BF16, kind="ExternalOutput").ap()
        d_x1 = nc.dram_tensor("d_x1", [2048, 1024], F32, kind="ExternalOutput").ap()
        d_x2 = nc.dram_tensor("d_x2", [2048, 1024], F32, kind="ExternalOutput").ap()
        d_att = nc.dram_tensor("d_att", [128, 1344], F32, kind="ExternalOutput").ap()
        d_pt = nc.dram_tensor("d_pt", [128, 512], BF16, kind="ExternalOutput").ap()
        d_v = nc.dram_tensor("d_v", [128, 130], BF16, kind="ExternalOutput").ap()
        d_q = nc.dram_tensor("d_q", [128, 512], BF16, kind="ExternalOutput").ap()
        d_k = nc.dram_tensor("d_k", [128, 512], BF16, kind="ExternalOutput").ap()

    with ExitStack() as st0:
        fw = FW(nc, st0)
        op = fw.op

        def ckpt(k):
            if stop == k:
                fw.barrier()
                raise _Stop(nc)
        bXB, bGX, bX1D, bX2D = Buf(), Buf(), Buf(), Buf()
        bXBs = Buf()

        uid = [0]

        def SB(st, name, shape, dt):
            uid[0] += 1
            return st.enter_context(nc.sbuf_tensor(f"{name}_{uid[0]}", shape, dt)), Buf()

        def PSB(st, name, shape, dt):
            uid[0] += 1
            return st.enter_context(nc.psum_tensor(f"{name}_{uid[0]}", shape, dt)), Buf()

        def mm(o, lhsT, rhs, start, stop, reads, writes):
            op("pe", lambda e: e.matmul(o, lhsT=lhsT, rhs=rhs, start=start, stop=stop), reads, writes, skip_self=True)

        def tr(o, in_, ident, reads, writes):
            op("pe", lambda e: e.transpose(out=o, in_=in_, identity=ident), reads, writes, skip_self=True)

        rr = [0]

        def cast(o, i, reads, writes, engs=("dve", "pool")):
            n = engs[rr[0] % len(engs)]
            rr[0] += 1
            if n == "act":
                op(n, lambda e: e.copy(out=o, in_=i), reads, writes)
            else:
                op(n, lambda e: e.tensor_copy(out=o, in_=i), reads, writes)

        ones_f, b_c = SB(st0, "ones_f", [128, 128], F32)
        ones_bf, _ = SB(st0, "ones_bf", [128, 128], BF16)
        ident_bf, _ = SB(st0, "ident_bf", [128, 128], BF16)
        ident_f, _ = SB(st0, "ident_f", [128, 128], F32)
        tri_f, _ = SB(st0, "tri_f", [128, 128], F32)
        tri_bf, _ = SB(st0, "tri_bf", [128, 128], BF16)
        negm_f, _ = SB(st0, "negm_f", [128, 128], F32)
        negc, _ = SB(st0, "negc", [128, 128], F32)
        sel_bf, _ = SB(st0, "sel_bf", [128, 2, 128], BF16)
        rp, _ = SB(st0, "rp", [128, NR], F32)
        lnt, blnt = SB(st0, "lnt", [128, 2, 1024], F32)
        small, b_sm = SB(st0, "small", [128, 64], F32)
        C = [b_c]
        op("pool", lambda e: e.memset(ones_f[:], 1.0), [], C)
        op("pool", lambda e: e.memset(ones_bf[:], 1.0), [], C)
        op("pool", lambda e: e.memset(negc[:], NEG), [], C)
        op("pool", lambda e: e.memset(sel_bf[:], 0.0), [], C)
        op("pool", lambda e: e.memset(sel_bf[0:64, 0, :], 1.0), C, C)
        op("pool", lambda e: e.memset(sel_bf[64:128, 1, :], 1.0), C, C)
        op("pool", lambda e: e.affine_select(out=ident_bf[:], in_=ones_f[:], pattern=[[-1, 128]], compare_op=ALU.is_equal, fill=0.0, base=0, channel_multiplier=1), C, C)
        op("pool", lambda e: e.affine_select(out=ident_f[:], in_=ones_f[:], pattern=[[-1, 128]], compare_op=ALU.is_equal, fill=0.0, base=0, channel_multiplier=1), C, C)
        op("pool", lambda e: e.affine_select(out=tri_f[:], in_=ones_f[:], pattern=[[1, 128]], compare_op=ALU.is_ge, fill=0.0, base=0, channel_multiplier=-1), C, C)
        op("pool", lambda e: e.affine_select(out=tri_bf[:], in_=ones_f[:], pattern=[[1, 128]], compare_op=ALU.is_ge, fill=0.0, base=0, channel_multiplier=-1), C, C)
        op("pool", lambda e: e.affine_select(out=negm_f[:], in_=negc[:], pattern=[[-1, 128]], compare_op=ALU.is_gt, fill=0.0, base=0, channel_multiplier=1), C, C)
        fw.dma("sp", lambda e: e.dma_start(out=rp[:], in_=rowp), [], C)
        DTB, ALOG, DSK, NORMW, SUBLN, LAMQ, LAMK = 0, 8, 16, 24, 536, 664, 792
        arow, _ = SB(st0, "arow", [128, 8], F32)
        subs, _ = SB(st0, "subs", [128, 128], F32)
        lamt, _ = SB(st0, "lamt", [128, 128], F32)
        op("act", lambda e: e.activation(out=arow[:], in_=rp[:, ALOG:ALOG + 8], func=AF.Exp), C, C)
        op("dve", lambda e: e.tensor_scalar_mul(out=arow[:], in0=arow[:], scalar1=-1.0), C, C)
        op("dve", lambda e: e.tensor_scalar_mul(out=subs[:], in0=rp[:, SUBLN:SUBLN + 128], scalar1=1.0 - LAMBDA_INIT), C, C)
        op("dve", lambda e: e.tensor_tensor(out=lamt[:], in0=rp[:, LAMQ:LAMQ + 128], in1=rp[:, LAMK:LAMK + 128], op=ALU.mult), C, C)
        op("dve", lambda e: e.reduce_sum(out=small[:, 0:1], in_=lamt[:, 0:64], axis=AX.X), C, [b_sm])
        op("dve", lambda e: e.reduce_sum(out=small[:, 1:2], in_=lamt[:, 64:128], axis=AX.X), C, [b_sm])
        op("act", lambda e: e.activation(out=small[:, 2:4], in_=small[:, 0:2], func=AF.Exp), [b_sm], [b_sm])
        op("dve", lambda e: e.tensor_tensor(out=small[:, 4:5], in0=small[:, 3:4], in1=small[:, 2:3], op=ALU.subtract), [b_sm], [b_sm])
        op("dve", lambda e: e.tensor_scalar_add(out=small[:, 5:6], in0=small[:, 4:5], scalar1=-LAMBDA_INIT), [b_sm], [b_sm])
        neglam = small[:, 5:6]

        def load_w_bf(st, name, src, ncols, c0=0, nk=8):
            wt, bw = SB(st, name, [128, nk, ncols], BF16)
            with ExitStack() as s2:
                cw_ = min(ncols, 1024)
                stg = [SB(s2, f"{name}_stg{i}", [128, cw_], F32) for i in range(2)]
                n = 0
                for kc in range(nk):
                    for cc in range(0, ncols, cw_):
                        w_ = min(cw_, ncols - cc)
                        t_, b_ = stg[n % 2]
                        n += 1
                        fw.dma("sp", lambda e: e.dma_start(out=t_[:, 0:w_], in_=src[kc * 128:(kc + 1) * 128, c0 + cc:c0 + cc + w_]), [], [b_])
                        cast(wt[:, kc, cc:cc + w_], t_[:, 0:w_], [b_], [bw])
                fw.barrier()
            return wt, bw

        def load_xT(st_bufs, src, row0, xT, bxT, PT, bPT, keep32=None, bkeep=None, cast_engs=("dve", "pool")):
            xs32, bxs32, xsbf, bxsbf = st_bufs
            for j in range(4):
                dst32 = keep32[:, j, :] if keep32 is not None else xs32[:]
                b32 = bkeep if keep32 is not None else bxs32
                fw.dma("sp", lambda e: e.dma_start(out=dst32, in_=src[row0 + j * 128:row0 + (j + 1) * 128, :]), [], [b32])
                cast(xsbf[:], dst32, [b32], [bxsbf], engs=cast_engs)
                for kc in range(8):
                    tr(PT[:, kc * 128:(kc + 1) * 128], xsbf[:, kc * 128:(kc + 1) * 128], ident_bf[:], [bxsbf] + C, [bPT])
                op("act", lambda e: e.copy(out=xT[:, :, j * 128:(j + 1) * 128], in_=PT[:].rearrange("p (k t) -> p k t", k=8)), [bPT], [bxT])

        def layer_norm(r, br, li, o, bo, tmp, btmp, st6, bst6):
            for hh in range(2):
                op("dve", lambda e: e.bn_stats(out=st6[:, hh * 6:(hh + 1) * 6], in_=r[:, hh * 512:(hh + 1) * 512]), [br], [bst6])
            op("dve", lambda e: e.bn_aggr(out=st6[:, 12:14], in_=st6[:, 0:12]), [bst6], [bst6])
            op("dve", lambda e: e.tensor_scalar_add(out=st6[:, 14:15], in0=st6[:, 13:14], scalar1=LN_EPS), [bst6], [bst6])
            op("act", lambda e: e.activation(out=st6[:, 15:16], in_=st6[:, 14:15], func=AF.Sqrt), [bst6], [bst6])
            op("dve", lambda e: e.reciprocal(out=st6[:, 16:17], in_=st6[:, 15:16]), [bst6], [bst6])
            op("dve", lambda e: e.tensor_scalar(out=tmp[:], in0=r[:], scalar1=st6[:, 12:13], scalar2=st6[:, 16:17], op0=ALU.subtract, op1=ALU.mult), [br, bst6], [btmp])
            op("pool", lambda e: e.tensor_tensor(out=tmp[:], in0=tmp[:], in1=lnt[:, 0, :], op=ALU.mult), [btmp, blnt], [btmp])
            op("pool", lambda e: e.tensor_tensor(out=o[:], in0=tmp[:], in1=lnt[:, 1, :], op=ALU.add), [btmp, blnt], [bo])

        bUVB = Buf()
        with ExitStack() as st:
            w1a, bw1a = load_w_bf(st, "w1a", w1, 1288, 0)
            PA, bPA = PSB(st, "PA", [128, 512], F32); PB, bPB = PSB(st, "PB", [128, 512], F32)
            PS0, bPS0 = PSB(st, "PS0", [128, 512], F32); PS1, bPS1 = PSB(st, "PS1", [128, 512], F32)
            PY, bPY = PSB(st, "PY", [128, 512], F32); PYo, bPYo = PSB(st, "PYo", [128, 512], F32)
            PT, bPT = PSB(st, "PT", [128, 1024], BF16); PM, bPM = PSB(st, "PM", [128, 512], F32)
            xs32, bxs32 = SB(st, "xs32", [128, 1024], F32); xsbf, bxsbf = SB(st, "xsbf", [128, 1024], BF16)
            xT, bxT = SB(st, "xT", [128, 8, 512], BF16)
            cwt, bcw = SB(st, "cwt", [128, 6, 4], F32); cbt, _ = SB(st, "cbt", [128, 6], F32)
            fw.dma("sp", lambda e: e.dma_start(out=cwt[:], in_=cw), [], [bcw])
            fw.dma("sp", lambda e: e.dma_start(out=cbt[:], in_=cbv), [], [bcw])
            halo, bhalo = SB(st, "halo", [128, 6, 3], F32)
            op("pool", lambda e: e.memset(halo[:], 0.0), [], [bhalo])
            xbc1, bxbc1 = SB(st, "xbc1", [128, 515], F32)
            acc, bacc = SB(st, "acc", [128, 512], F32)
            fm, bfm = SB(st, "fm", [128, 6, 512], BF16)
            tm, btm = SB(st, "tm", [128, 4, 640], BF16)
            zs, bzs = SB(st, "zs", [128, 512], F32)
            sm, bsm = SB(st, "sm", [128, 96], F32)
            H, bH = SB(st, "H", [128, 512], F32); Hbf, bHbf = SB(st, "Hbf", [128, 512], BF16)
            op("pool", lambda e: e.memset(H[:], 0.0), [], [bH])
            op("pool", lambda e: e.memset(Hbf[:], 0.0), [], [bHbf])
            X, bX = SB(st, "X", [128, 512], BF16); Xd, bXd = SB(st, "Xd", [128, 512], BF16)
            Abc, bAbc = SB(st, "Abc", [128, 8, 128], F32); Eall, bE = SB(st, "Eall", [128, 8, 128], F32)
            CBT, bCBT = SB(st, "CBT", [128, 128], F32); MT, bMT = SB(st, "MT", [128, 8, 128], BF16)
            t1, bt1 = SB(st, "t1", [128, 512], F32); t2, bt2 = SB(st, "t2", [128, 512], F32); t3, bt3 = SB(st, "t3", [128, 512], F32)
            yn, byn = SB(st, "yn", [128, 512], BF16); ygT, bygT = SB(st, "ygT", [128, 4, 512], BF16)
            v3 = lambda ap: ap.rearrange("p (h d) -> p h d", h=8)
            bc3 = lambda ap: ap.unsqueeze(2).broadcast_to([128, 8, 64])
            for g in range(16 if stop != 12 else 0):
                load_xT((xs32, bxs32, xsbf, bxsbf), xb, g * 512, xT, bxT, PT, bPT)
                for blk in range(6):
                    for kc in range(8):
                        mm(PA[:], w1a[:, kc, blk * 128:(blk + 1) * 128], xT[:, kc, :], kc == 0, kc == 7, [bw1a, bxT], [bPA])
                    op("pool", lambda e: e.tensor_copy(out=xbc1[:, 0:3], in_=halo[:, blk, :]), [bhalo], [bxbc1])
                    op("act", lambda e: e.copy(out=xbc1[:, 3:515], in_=PA[:]), [bPA], [bxbc1])
                    op("pool", lambda e: e.tensor_copy(out=halo[:, blk, :], in_=xbc1[:, 512:515]), [bxbc1], [bhalo])
                    op("dve", lambda e: e.tensor_scalar_mul(out=acc[:], in0=xbc1[:, 0:512], scalar1=cwt[:, blk, 0:1]), [bxbc1, bcw], [bacc])
                    for k in range(1, 4):
                        op("dve", lambda e: e.scalar_tensor_tensor(out=acc[:], in0=xbc1[:, k:k + 512], scalar=cwt[:, blk, k:k + 1], in1=acc[:], op0=ALU.mult, op1=ALU.add), [bxbc1, bcw, bacc], [bacc])
                    op("act", lambda e: e.activation(out=fm[:, blk, :], in_=acc[:], func=AF.Silu, bias=cbt[:, blk:blk + 1]), [bacc, bcw], [bfm])
                for j in range(4):
                    for blk in range(5):
                        tr(PT[:, blk * 128:(blk + 1) * 128], fm[:, blk, j * 128:(j + 1) * 128], ident_bf[:], [bfm] + C, [bPT])
                    op("act", lambda e: e.copy(out=tm[:, j, :], in_=PT[:, 0:640]), [bPT], [btm])
                for j in range(4):
                    ts = slice(j * 128, (j + 1) * 128)
                    for kc in range(8):
                        mm(PA[:], xT[:, kc, ts], w1a[:, kc, 768:1280], kc == 0, kc == 7, [bw1a, bxT], [bPA])
                    for kc in range(8):
                        mm(PM[:, 0:8], xT[:, kc, ts], w1a[:, kc, 1280:1288], kc == 0, kc == 7, [bw1a, bxT], [bPM])
                    op("act", lambda e: e.activation(out=zs[:], in_=PA[:], func=AF.Silu), [bPA], [bzs])
                    op("dve", lambda e: e.tensor_tensor(out=sm[:, 0:8], in0=PM[:, 0:8], in1=rp[:, DTB:DTB + 8], op=ALU.add), [bPM] + C, [bsm])
                    op("dve", lambda e: e.tensor_scalar_mul(out=sm[:, 8:16], in0=sm[:, 0:8], scalar1=-1.0), [bsm], [bsm])
                    op("dve", lambda e: e.tensor_tensor(out=sm[:, 8:16], in0=sm[:, 8:16], in1=sm[:, 0:8], op=ALU.max), [bsm], [bsm])
                    op("act", lambda e: e.activation(out=sm[:, 8:16], in_=sm[:, 8:16], func=AF.Exp, scale=-1.0), [bsm], [bsm])
                    op("act", lambda e: e.activation(out=sm[:, 8:16], in_=sm[:, 8:16], func=AF.Ln, bias=1.0), [bsm], [bsm])
                    op("dve", lambda e: e.tensor_scalar_max(out=sm[:, 0:8], in0=sm[:, 0:8], scalar1=0.0), [bsm], [bsm])
                    op("dve", lambda e: e.tensor_tensor(out=sm[:, 16:24], in0=sm[:, 0:8], in1=sm[:, 8:16], op=ALU.add), [bsm], [bsm])
                    op("dve", lambda e: e.tensor_tensor(out=sm[:, 24:32], in0=sm[:, 16:24], in1=arow[:], op=ALU.mult), [bsm] + C, [bsm])
                    mm(PM[:, 8:16], tri_f[:], sm[:, 24:32], True, True, [bsm] + C, [bPM])
                    mm(PM[:, 16:24], ones_f[:], sm[:, 24:32], True, True, [bsm] + C, [bPM])
                    op("dve", lambda e: e.tensor_scalar_mul(out=sm[:, 32:40], in0=PM[:, 8:16], scalar1=-1.0), [bPM], [bsm])
                    op("act", lambda e: e.activation(out=sm[:, 40:48], in_=PM[:, 8:16], func=AF.Exp), [bPM], [bsm])
                    op("dve", lambda e: e.tensor_tensor(out=sm[:, 48:56], in0=PM[:, 16:24], in1=sm[:, 32:40], op=ALU.add), [bPM, bsm], [bsm])
                    op("act", lambda e: e.activation(out=sm[:, 48:56], in_=sm[:, 48:56], func=AF.Exp), [bsm], [bsm])
                    op("act", lambda e: e.activation(out=sm[:, 56:64], in_=PM[:, 16:24], func=AF.Exp), [bPM], [bsm])
                    op("dve", lambda e: e.tensor_tensor(out=v3(X[:]), in0=v3(tm[:, j, 0:512]), in1=bc3(sm[:, 16:24]), op=ALU.mult), [btm, bsm], [bX])
                    op("pool", lambda e: e.tensor_tensor(out=v3(Xd[:]), in0=v3(X[:]), in1=bc3(sm[:, 48:56]), op=ALU.mult), [bX, bsm], [bXd])
                    op("dve", lambda e: e.tensor_copy(out=Abc[:], in_=sm[:, 24:32].unsqueeze(2).broadcast_to([128, 8, 128])), [bsm], [bAbc])
                    for half in range(2):
                        PSh, bPSh = (PS0, bPS0) if half == 0 else (PS1, bPS1)
                        for hq in range(4):
                            h = half * 4 + hq
                            mm(PSh[:, hq * 128:(hq + 1) * 128], Abc[:, h, :], tri_f[:], True, False, [bAbc] + C, [bPSh])
                            mm(PSh[:, hq * 128:(hq + 1) * 128], ident_f[:], negm_f[:], False, True, C, [bPSh])
                        for hq in range(4):
                            h = half * 4 + hq
                            op("act", lambda e: e.activation(out=Eall[:, h, :], in_=PSh[:, hq * 128:(hq + 1) * 128], func=AF.Exp, bias=sm[:, 32 + h:33 + h]), [bPSh, bsm], [bE])
                    mm(PB[:, 0:128], fm[:, 4, ts], fm[:, 5, ts], True, True, [bfm], [bPB])
                    op("dve", lambda e: e.tensor_copy(out=CBT[:], in_=PB[:, 0:128]), [bPB], [bCBT])
                    op("dve", lambda e: e.tensor_tensor(out=MT[:], in0=Eall[:], in1=CBT[:].unsqueeze(1).broadcast_to([128, 8, 128]), op=ALU.mult), [bE, bCBT], [bMT])
                    for h in range(8):
                        mm(PY[:, h * 64:(h + 1) * 64], MT[:, h, :], X[:, h * 64:(h + 1) * 64], True, True, [bMT, bX], [bPY])
                    mm(PYo[:], fm[:, 5, ts], Hbf[:], True, True, [bfm, bHbf], [bPYo])
                    op("dve", lambda e: e.tensor_tensor(out=v3(t1[:]), in0=v3(PYo[:]), in1=bc3(sm[:, 40:48]), op=ALU.mult), [bPYo, bsm], [bt1])
                    op("dve", lambda e: e.tensor_tensor(out=t2[:], in0=t1[:], in1=PY[:], op=ALU.add), [bt1, bPY], [bt2])
                    op("pool", lambda e: e.tensor_tensor(out=v3(t3[:]), in0=v3(tm[:, j, 0:512]), in1=bc3(rp[:, DSK:DSK + 8]), op=ALU.mult), [btm] + C, [bt3])
                    op("pool", lambda e: e.tensor_tensor(out=t3[:], in0=t3[:], in1=t2[:], op=ALU.add), [bt3, bt2], [bt3])
                    op("dve", lambda e: e.tensor_tensor(out=t1[:], in0=t3[:], in1=zs[:], op=ALU.mult), [bt3, bzs], [bt1])
                    op("pool", lambda e: e.memset(sm[:, 64:65], 0.0), [], [bsm])
                    op("act", lambda e: e.activation(out=t2[:], in_=t1[:], func=AF.Square, accum_out=sm[:, 64:65]), [bt1], [bt2, bsm])
                    op("dve", lambda e: e.tensor_scalar(out=sm[:, 65:66], in0=sm[:, 64:65], scalar1=1.0 / 512.0, scalar2=RMS_EPS, op0=ALU.mult, op1=ALU.add), [bsm], [bsm])
                    op("act", lambda e: e.activation(out=sm[:, 66:67], in_=sm[:, 65:66], func=AF.Sqrt), [bsm], [bsm])
                    op("dve", lambda e: e.reciprocal(out=sm[:, 67:68], in_=sm[:, 66:67]), [bsm], [bsm])
                    op("dve", lambda e: e.scalar_tensor_tensor(out=yn[:], in0=t1[:], scalar=sm[:, 67:68], in1=rp[:, NORMW:NORMW + 512], op0=ALU.mult, op1=ALU.mult), [bt1, bsm] + C, [byn])
                    for cb in range(4):
                        tr(PT[:, cb * 128:(cb + 1) * 128], yn[:, cb * 128:(cb + 1) * 128], ident_bf[:], [byn] + C, [bPT])
                    op("act", lambda e: e.copy(out=ygT[:, :, ts], in_=PT[:, 0:512].rearrange("p (c t) -> p c t", c=4)), [bPT], [bygT])
                    mm(PB[:], tm[:, j, 512:640], Xd[:], True, True, [btm, bXd], [bPB])
                    op("dve", lambda e: e.tensor_tensor(out=v3(H[:]), in0=v3(H[:]), in1=bc3(sm[:, 56:64]), op=ALU.mult), [bH, bsm], [bH])
                    op("dve", lambda e: e.tensor_tensor(out=H[:], in0=H[:], in1=PB[:], op=ALU.add), [bH, bPB], [bH])
                    op("act", lambda e: e.copy(out=Hbf[:], in_=H[:]), [bH], [bHbf])
                fw.dma("sp", lambda e: e.dma_start(out=XB[0:512, g * 512:(g + 1) * 512].rearrange("(c p) t -> p c t", p=128), in_=ygT[:]), [bygT], [bXBs])
            fw.barrier()

        def exchange(i):
            fw.dma("pool", lambda e: e.collective_compute("AllGather", ALU.bypass, replica_groups=[[0, 1, 2, 3], [4, 5, 6, 7]], ins=[XB[i * 64:(i + 1) * 64, :].rearrange("p (a b) -> (p a) b", b=1024)], outs=[GX[i * 256:(i + 1) * 256, :].rearrange("p (a b) -> (p a) b", b=1024)]), [bXBs if i < 8 else bXB], [bGX], inc=1, slotq="cc", nslots=12)

        if stop > 1:
            for i in range(8):
                exchange(i)

        ckpt(1)
        with ExitStack() as st:
            w1b, bw1b = load_w_bf(st, "w1b", w1, 768, 1288)
            PA, bPA = PSB(st, "PA", [128, 512], F32)
            PS = [PSB(st, f"PS{i}", [128, 512], F32) for i in range(2)]
            PO = [PSB(st, f"PO{i}", [128, 512], F32) for i in range(4)]
            PT, bPT = PSB(st, "PT", [128, 1024], BF16)
            xs32, bxs32 = SB(st, "xs32", [128, 1024], F32); xsbf, bxsbf = SB(st, "xsbf", [128, 1024], BF16)
            xT, bxT = SB(st, "xT", [128, 8, 512], BF16)
            kT, _ = SB(st, "kT", [128, 2, 8192], BF16); bkT = [Buf() for _ in range(16)]
            Vall, _ = SB(st, "Vall", [128, 64, 2, 130], BF16); bV = [Buf() for _ in range(16)]
            bVone = Buf()
            op("pool", lambda e: e.memset(Vall[:, :, :, 128:129], 1.0), [], bV)
            qT, bqT = SB(st, "qT", [128, 2, 512], BF16)
            q32, bq32 = SB(st, "q32", [128, 512], F32)
            sq, bsq = SB(st, "sq", [128, 2, 512], BF16)
            nb, bnb = SB(st, "nb", [128, 24], F32)
            op("pool", lambda e: e.memset(nb[:], 0.0), [], [bnb])
            PTt = [SB(st, f"PTt{i}", [128, 512], BF16) for i in range(3)]
            o1, bo1 = SB(st, "o1", [128, 4, 128], F32)
            oo, boo = SB(st, "oo", [128, 4, 128], F32)
            junk, bjunk = SB(st, "junk", [128, 128], F32)
            on, bon = SB(st, "on", [128, 128], BF16)
            oT, boT = SB(st, "oT", [128, 2, 512], BF16)
            s2, bs2 = SB(st, "s2", [128, 16], F32)
            cnt = 0
            s32 = [SB(st, f"uv32_{i}", [128, 2048], F32) for i in range(3)]
            s16 = [SB(st, f"uv16_{i}", [128, 2048], BF16) for i in range(3)]

            def precast(i):
                a32, ba32 = s32[i % 3]
                a16, ba16 = s16[i % 3]
                fw.dma("pool", lambda e: e.dma_start(out=a32[:], in_=peer_uv[i * 128:(i + 1) * 128, :]), [], [ba32])
                op("dve", lambda e: e.tensor_copy(out=a16[:], in_=a32[:]), [ba32], [ba16])
                fw.dma("pool", lambda e: e.dma_start(out=UVB[i * 128:(i + 1) * 128, :], in_=a16[:]), [ba16], [bUVB])

            for g in range(16 if stop != 12 else 1):
                load_xT((xs32, bxs32, xsbf, bxsbf), xb, g * 512, xT, bxT, PT, bPT, cast_engs=("dve",))
                for i in range(8 * g, 8 * g + 8):
                    precast(i)
                cs = slice(g * 512, (g + 1) * 512)
                for hh in range(2):
                    for kc in range(8):
                        mm(PA[:], w1b[:, kc, 256 + hh * 128:256 + (hh + 1) * 128], xT[:, kc, :], kc == 0, kc == 7, [bw1b, bxT], [bPA])
                    op("dve", lambda e: e.tensor_copy(out=kT[:, hh, cs], in_=PA[:]), [bPA], [bkT[g]])
                    for kc in range(8):
                        mm(PA[:], w1b[:, kc, hh * 128:(hh + 1) * 128], xT[:, kc, :], kc == 0, kc == 7, [bw1b, bxT], [bPA])
                    op("dve", lambda e: e.tensor_copy(out=q32[:], in_=PA[:]), [bPA], [bq32])
                    op("dve", lambda e: e.tensor_scalar_mul(out=qT[:, hh, :], in0=q32[:], scalar1=0.125), [bq32], [bqT])
                    if DBG and g == 0 and hh == 0:
                        fw.dma("sp", lambda e: e.dma_start(out=d_pt, in_=qT[:, 0, :]), [bqT], [])
                for j in range(4):
                    for kc in range(8):
                        mm(PA[:, 0:256], xT[:, kc, j * 128:(j + 1) * 128], w1b[:, kc, 512:768], kc == 0, kc == 7, [bw1b, bxT], [bPA])
                    op("dve", lambda e: e.tensor_copy(out=Vall[:, g * 4 + j, :, 0:128], in_=PA[:, 0:256].rearrange("p (h d) -> p h d", h=2)), [bPA], [bV[g]])
                op("dve", lambda e: e.tensor_tensor(out=sq[:], in0=qT[:], in1=qT[:], op=ALU.mult), [bqT], [bsq])
                for hh in range(2):
                    for m in range(2):
                        mm(PA[:], sel_bf[:, m, :], sq[:, hh, :], True, True, [bsq] + C, [bPA])
                        op("dve", lambda e: e.reduce_max(out=nb[:, hh * 2 + m:hh * 2 + m + 1], in_=PA[:], axis=AX.X), [bPA], [bnb])
                op("dve", lambda e: e.tensor_tensor(out=sq[:], in0=kT[:, :, cs], in1=kT[:, :, cs], op=ALU.mult), [bkT[g]], [bsq])
                for hh in range(2):
                    for m in range(2):
                        mm(PA[:], sel_bf[:, m, :], sq[:, hh, :], True, True, [bsq] + C, [bPA])
                        op("dve", lambda e: e.reduce_max(out=nb[:, 4 + hh * 2 + m:5 + hh * 2 + m], in_=PA[:], axis=AX.X), [bPA], [bnb])
                op("dve", lambda e: e.tensor_tensor(out=nb[:, 8:12], in0=nb[:, 8:12], in1=nb[:, 4:8], op=ALU.max), [bnb], [bnb])
                op("dve", lambda e: e.tensor_tensor(out=nb[:, 12:16], in0=nb[:, 8:12], in1=nb[:, 0:4], op=ALU.mult), [bnb], [bnb])
                op("act", lambda e: e.activation(out=nb[:, 12:16], in_=nb[:, 12:16], func=AF.Sqrt), [bnb], [bnb])
                op("dve", lambda e: e.tensor_scalar_mul(out=nb[:, 16:20], in0=nb[:, 12:16], scalar1=-1.0), [bnb], [bnb])
                for hh in range(2):
                    for m in range(2):
                        ms = slice(m * 64, (m + 1) * 64)
                        nkb = 4 * g + 4

                        def emit_S(kb, u):
                            r = max(0, kb - 4 * g)
                            Ps, bPs = PS[u % 2]
                            mm(Ps[:, r * 128:512], kT[ms, hh, kb * 128:(kb + 1) * 128], qT[ms, hh, r * 128:512], True, True, [bkT[kb // 4], bqT], [bPs])

                        emit_S(0, cnt)
                        for kb in range(nkb):
                            r = max(0, kb - 4 * g)
                            Ps, bPs = PS[cnt % 2]
                            Pt, bPt = PTt[cnt % 3]
                            if kb + 1 < nkb:
                                emit_S(kb + 1, cnt + 1)
                            cnt += 1
                            op("act", lambda e: e.activation(out=Pt[:, r * 128:512], in_=Ps[:, r * 128:512], func=AF.Exp, bias=nb[:, 16 + hh * 2 + m:17 + hh * 2 + m]), [bPs, bnb], [bPt])
                            if kb >= 4 * g:
                                op("dve", lambda e: e.tensor_tensor(out=Pt[:, r * 128:(r + 1) * 128], in0=Pt[:, r * 128:(r + 1) * 128], in1=tri_bf[:], op=ALU.mult), [bPt] + C, [bPt])
                            for qb in range(r, 4):
                                mm(PO[qb][0][:, 0:129], Pt[:, qb * 128:(qb + 1) * 128], Vall[:, kb, hh, 0:129], kb == 0, kb == 4 * g + qb, [bPt, bV[kb // 4]], [PO[qb][1]])
                        for qb in range(4):
                            Pq, bPq = PO[qb]
                            op("dve", lambda e: e.reciprocal(out=s2[:, 0:1], in_=Pq[:, 128:129]), [bPq], [bs2])
                            if m == 0:
                                op("dve", lambda e: e.tensor_scalar_mul(out=o1[:, qb, :], in0=Pq[:, 0:128], scalar1=s2[:, 0:1]), [bPq, bs2], [bo1])
                            else:
                                op("dve", lambda e: e.tensor_tensor(out=s2[:, 1:2], in0=s2[:, 0:1], in1=neglam, op=ALU.mult), [bs2, b_sm], [bs2])
                                op("dve", lambda e: e.scalar_tensor_tensor(out=oo[:, qb, :], in0=Pq[:, 0:128], scalar=s2[:, 1:2], in1=o1[:, qb, :], op0=ALU.mult, op1=ALU.add), [bPq, bs2, bo1], [boo])
                    for qb in range(4):
                        op("dve", lambda e: e.memset(s2[:, 4:5], 0.0), [], [bs2])
                        op("act", lambda e: e.activation(out=junk[:], in_=oo[:, qb, :], func=AF.Square, accum_out=s2[:, 4:5]), [boo], [bjunk, bs2])
                        op("dve", lambda e: e.tensor_scalar(out=s2[:, 5:6], in0=s2[:, 4:5], scalar1=1.0 / 128.0, scalar2=RMS_EPS, op0=ALU.mult, op1=ALU.add), [bs2], [bs2])
                        op("act", lambda e: e.activation(out=s2[:, 6:7], in_=s2[:, 5:6], func=AF.Sqrt), [bs2], [bs2])
                        op("dve", lambda e: e.reciprocal(out=s2[:, 7:8], in_=s2[:, 6:7]), [bs2], [bs2])
                        op("dve", lambda e: e.scalar_tensor_tensor(out=on[:], in0=oo[:, qb, :], scalar=s2[:, 7:8], in1=subs[:], op0=ALU.mult, op1=ALU.mult), [boo, bs2] + C, [bon])
                        tr(PT[:, qb * 128:(qb + 1) * 128], on[:], ident_bf[:], [bon] + C, [bPT])
                    op("act", lambda e: e.copy(out=oT[:, hh, :], in_=PT[:, 0:512]), [bPT], [boT])
                    if DBG and g == 0 and hh == 0:
                        fw.dma("sp", lambda e: e.dma_start(out=d_att[:, 0:24], in_=nb[:]), [bnb], [])
                        fw.dma("sp", lambda e: e.dma_start(out=d_att[:, 24:40], in_=s2[:]), [bs2], [])
                        fw.dma("sp", lambda e: e.dma_start(out=d_att[:, 576:1088], in_=oo[:].rearrange("p a b -> p (a b)")), [boo], [])
                        fw.dma("sp", lambda e: e.dma_start(out=d_att[:, 1088:1344], in_=qT[:, 0, 0:128].bitcast(F32) if False else small[:, 0:1].broadcast_to([128, 256])), [b_sm], []) if False else None
                fw.dma("sp", lambda e: e.dma_start(out=XB[512:768, cs].rearrange("(h p) t -> p h t", p=128), in_=oT[:]), [boT], [bXB])
            fw.barrier()

        if DBG:
            fw.dma("sp", lambda e: e.dma_start(out=d_xb, in_=XB), [bXB, bXBs], [])
        ckpt(2)
        ckpt(12)
        for i in range(8, 12):
            exchange(i)
        GXV = GX.rearrange("r (q t) -> (r q) t", t=256)
        ckpt(3)

        with ExitStack() as stP2:
            st6, bst6 = SB(stP2, "st6", [128, 24], F32)
            xsbf, bxsbf = SB(stP2, "xsbf2", [128, 1024], BF16)
            xs32, bxs32 = SB(stP2, "xs32b", [128, 1024], F32)
            rres, brres = SB(stP2, "rres", [128, 1024], F32)
            lno, blno = SB(stP2, "lno", [128, 1024], F32)
            ltmp, bltmp = SB(stP2, "ltmp", [128, 1024], F32)
            idxT_t, bidxT = SB(stP2, "idxT_t", [128, 128], I32)
            gateT_t, bgateT = SB(stP2, "gateT_t", [128, 128], F32)
            bIDXD, bGATED = Buf(), Buf()
            print("sbuf remaining (P2 start):", nc.sbuf_bytes_remaining)

            def load_ln(li):
                fw.dma("sp", lambda e: e.dma_start(out=lnt[:], in_=lnp[:, 2 * li:2 * li + 2, :]), [], [blnt])

            def to_featT(src, bsrc, dstT, bdstT, col0, PT, bPT):
                cast(xsbf[:], src, [bsrc], [bxsbf])
                for kc in range(8):
                    tr(PT[:, kc * 128:(kc + 1) * 128], xsbf[:, kc * 128:(kc + 1) * 128], ident_bf[:], [bxsbf] + C, [bPT])
                op("act", lambda e: e.copy(out=dstT[:, :, col0:col0 + 128], in_=PT[:].rearrange("p (k t) -> p k t", k=8)), [bPT], [bdstT])

            with ExitStack() as stX:
                xfT, bxfT = SB(stX, "xfT", [128, 8, 2048], BF16)
                with ExitStack() as st2A:
                    mixT, bmixT = SB(st2A, "mixT", [128, 8, 2048], BF16)
                    with ExitStack() as st:
                        wsb, bwsb = load_w_bf(st, "wsb", w_ssd, 1024, 0, nk=16)
                        wdb, bwdb = load_w_bf(st, "wdb", w_diff, 1024, 0)
                        wgb, bwgb = load_w_bf(st, "wgb", wg, 2048, 0)
                        gbt, bgbt = SB(st, "gbt", [128, 16], F32)
                        fw.dma("sp", lambda e: e.dma_start(out=gbt[:], in_=gbias), [], [bgbt])
                        idt, bidt = SB(st, "idt", [128, 24, 8], I32)
                        fw.dma("sp", lambda e: e.dma_start(out=idt[:], in_=idq), [], [bidt])
                        PA, bPA = PSB(st, "PA", [128, 512], F32); PB, bPB = PSB(st, "PB", [128, 512], F32)
                        PY, bPY = PSB(st, "PY", [128, 512], F32); PYo, bPYo = PSB(st, "PYo", [128, 512], F32)
                        PT, bPT = PSB(st, "PT", [128, 1024], BF16)
                        GXt, bGXt = SB(st, "GXt", [128, 24, 256], BF16)
                        xT, bxT = SB(st, "xT", [128, 8, 256], BF16)
                        sa, bsa = SB(st, "sa", [128, 256], F32); sbb, bsbb = SB(st, "sbb", [128, 256], F32)
                        print("sbuf remaining (2A-1):", nc.sbuf_bytes_remaining)
                        for tt in range(8):
                            tcs = slice(tt * 256, (tt + 1) * 256)
                            for kb in range(24):
                                fw.dma("pool", lambda e: e.indirect_dma_start(out=GXt[:, kb, :], out_offset=None, in_=GXV, in_offset=bass.IndirectOffsetOnAxis(ap=idt[:, kb, tt:tt + 1], axis=0)), [bidt, bGX], [bGXt])
                            for j in range(2):
                                row0 = tt * 256 + j * 128
                                fw.dma("sp", lambda e: e.dma_start(out=xs32[:], in_=xown[row0:row0 + 128, :]), [], [bxs32])
                                to_featT(xs32[:], bxs32, xT, bxT, j * 128, PT, bPT)
                            for fb in range(8):
                                fs = slice(fb * 128, (fb + 1) * 128)
                                for kc in range(8):
                                    mm(PA[:, 0:256], wgb[:, kc, fs], xT[:, kc, :], kc == 0, kc == 7, [bwgb, bxT], [bPA])
                                for kc in range(8):
                                    mm(PB[:, 0:256], wgb[:, kc, 1024 + fb * 128:1024 + (fb + 1) * 128], xT[:, kc, :], kc == 0, kc == 7, [bwgb, bxT], [bPB])
                                i = 0
                                for r in range(4):
                                    for blk in range(4):
                                        mm(PY[:, 0:256], wsb[:, r * 4 + blk, fs], GXt[:, r * 6 + blk, :], i == 0, i == 15, [bwsb, bGXt], [bPY])
                                        i += 1
                                i = 0
                                for r in range(4):
                                    for hh in range(2):
                                        mm(PYo[:, 0:256], wdb[:, r * 2 + hh, fs], GXt[:, r * 6 + 4 + hh, :], i == 0, i == 7, [bwdb, bGXt], [bPYo])
                                        i += 1
                                op("act", lambda e: e.activation(out=sa[:], in_=PA[:, 0:256], func=AF.Sigmoid, bias=gbt[:, fb:fb + 1]), [bPA, bgbt], [bsa])
                                op("act", lambda e: e.activation(out=sbb[:], in_=PB[:, 0:256], func=AF.Sigmoid, bias=gbt[:, 8 + fb:9 + fb]), [bPB, bgbt], [bsbb])
                                op("dve", lambda e: e.tensor_tensor(out=sa[:], in0=sa[:], in1=PY[:, 0:256], op=ALU.mult), [bsa, bPY], [bsa])
                                op("dve", lambda e: e.tensor_tensor(out=sbb[:], in0=sbb[:], in1=PYo[:, 0:256], op=ALU.mult), [bsbb, bPYo], [bsbb])
                                op("pool", lambda e: e.tensor_tensor(out=mixT[:, fb, tcs], in0=sa[:], in1=sbb[:], op=ALU.add), [bsa, bsbb], [bmixT])
                        fw.barrier()
                    ckpt(4)
                    with ExitStack() as st:
                        wob, bwob = load_w_bf(st, "wob", w_o, 1024, 0)
                        load_ln(0)
                        PS0, bPS0 = PSB(st, "PS0", [128, 512], F32); PS1, bPS1 = PSB(st, "PS1", [128, 512], F32)
                        PT, bPT = PSB(st, "PT", [128, 1024], BF16)
                        for tk in range(16):
                            row0 = tk * 128
                            fw.dma("sp", lambda e: e.dma_start(out=xs32[:], in_=xown[row0:row0 + 128, :]), [], [bxs32])
                            for half, (Ph, bPh) in enumerate(((PS0, bPS0), (PS1, bPS1))):
                                for fb in range(8):
                                    mm(Ph[:], mixT[:, fb, row0:row0 + 128], wob[:, fb, half * 512:(half + 1) * 512], fb == 0, fb == 7, [bmixT, bwob], [bPh])
                                op("dve", lambda e: e.scalar_tensor_tensor(out=rres[:, half * 512:(half + 1) * 512], in0=xs32[:, half * 512:(half + 1) * 512], scalar=ALPHA, in1=Ph[:], op0=ALU.mult, op1=ALU.add), [bxs32, bPh], [brres])
                            layer_norm(rres, brres, 0, lno, blno, ltmp, bltmp, st6, bst6)
                            fw.dma("sp", lambda e: e.dma_start(out=X1D[row0:row0 + 128, :], in_=lno[:]), [blno], [bX1D])
                            if DBG:
                                fw.dma("sp", lambda e: e.dma_start(out=d_x1[row0:row0 + 128, :], in_=lno[:]), [blno], [])
                            to_featT(lno[:], blno, xfT, bxfT, row0, PT, bPT)
                        fw.barrier()

                ckpt(5)
                with ExitStack() as st:
                    wcq, bwcq = load_w_bf(st, "wcq", w_cq, 1024)
                    wck, bwck = load_w_bf(st, "wck", w_ck, 1024)
                    wcv, bwcv = load_w_bf(st, "wcv", w_cv, 1024)
                    wco, bwco = load_w_bf(st, "wco", w_co, 1024)
                    load_ln(1)
                    PA, bPA = PSB(st, "PA", [128, 512], F32); PB, bPB = PSB(st, "PB", [128, 512], F32)
                    PS0, bPS0 = PSB(st, "PS0", [128, 512], F32); PS1, bPS1 = PSB(st, "PS1", [128, 512], F32)
                    PO_, bPO_ = PSB(st, "PO", [128, 512], F32); PZ, bPZ = PSB(st, "PZ", [128, 512], F32)
                    PT, bPT = PSB(st, "PT", [128, 1024], BF16)
                    memT, bmemT = SB(st, "memT", [128, 8, 256], BF16)
                    kcT, bkcT = SB(st, "kcT", [128, 8, 256], BF16)
                    vc, bvc = SB(st, "vc", [128, 2, 1024], BF16)
                    qcT, bqcT = SB(st, "qcT", [128, 8, 256], BF16)
                    sqc, bsqc = SB(st, "sqc", [128, 2, 256], BF16)
                    nbc, bnbc = SB(st, "nbc", [128, 16], F32)
                    Pc, bPc = SB(st, "Pc", [128, 2, 256], BF16)
                    rz, brz = SB(st, "rz", [128, 256], F32)
                    ocT, bocT = SB(st, "ocT", [128, 8, 256], BF16)
                    print("sbuf remaining (2B):", nc.sbuf_bytes_remaining)
                    for mb in range(2):
                        fw.dma("sp", lambda e: e.dma_start(out=xs32[:], in_=memb[mb * 128:(mb + 1) * 128, :]), [], [bxs32])
                        to_featT(xs32[:], bxs32, memT, bmemT, mb * 128, PT, bPT)
                    for fb in range(8):
                        for kc in range(8):
                            mm(PA[:, 0:256], wck[:, kc, fb * 128:(fb + 1) * 128], memT[:, kc, :], kc == 0, kc == 7, [bwck, bmemT], [bPA])
                        op("dve", lambda e: e.tensor_copy(out=kcT[:, fb, :], in_=PA[:, 0:256]), [bPA], [bkcT])
                    for mb in range(2):
                        for half in range(2):
                            for kc in range(8):
                                mm(PA[:], memT[:, kc, mb * 128:(mb + 1) * 128], wcv[:, kc, half * 512:(half + 1) * 512], kc == 0, kc == 7, [bwcv, bmemT], [bPA])
                            op("dve", lambda e: e.tensor_copy(out=vc[:, mb, half * 512:(half + 1) * 512], in_=PA[:]), [bPA], [bvc])
                    for h in range(4):
                        op("dve", lambda e: e.tensor_tensor(out=sqc[:], in0=kcT[:, 2 * h:2 * h + 2, :], in1=kcT[:, 2 * h:2 * h + 2, :], op=ALU.mult), [bkcT], [bsqc])
                        for ch in range(2):
                            mm(PA[:, 0:256], ones_bf[:], sqc[:, ch, :], ch == 0, ch == 1, [bsqc] + C, [bPA])
                        op("dve", lambda e: e.reduce_max(out=nbc[:, h:h + 1], in_=PA[:, 0:256], axis=AX.X), [bPA], [bnbc])
                    for tt in range(8):
                        tcs = slice(tt * 256, (tt + 1) * 256)
                        for fb in range(8):
                            for kc in range(8):
                                mm(PA[:, 0:256], wcq[:, kc, fb * 128:(fb + 1) * 128], xfT[:, kc, tcs], kc == 0, kc == 7, [bwcq, bxfT], [bPA])
                            op("dve", lambda e: e.tensor_scalar_mul(out=qcT[:, fb, :], in0=PA[:, 0:256], scalar1=1.0 / 16.0), [bPA], [bqcT])
                        for h in range(4):
                            op("dve", lambda e: e.tensor_tensor(out=sqc[:], in0=qcT[:, 2 * h:2 * h + 2, :], in1=qcT[:, 2 * h:2 * h + 2, :], op=ALU.mult), [bqcT], [bsqc])
                            for ch in range(2):
                                mm(PB[:, 0:256], ones_bf[:], sqc[:, ch, :], ch == 0, ch == 1, [bsqc] + C, [bPB])
                            op("dve", lambda e: e.reduce_max(out=nbc[:, 4:5], in_=PB[:, 0:256], axis=AX.X), [bPB], [bnbc])
                            op("dve", lambda e: e.tensor_tensor(out=nbc[:, 5:6], in0=nbc[:, 4:5], in1=nbc[:, h:h + 1], op=ALU.mult), [bnbc], [bnbc])
                            op("act", lambda e: e.activation(out=nbc[:, 6:7], in_=nbc[:, 5:6], func=AF.Sqrt), [bnbc], [bnbc])
                            op("dve", lambda e: e.tensor_scalar_mul(out=nbc[:, 7:8], in0=nbc[:, 6:7], scalar1=-1.0), [bnbc], [bnbc])
                            for mb, (Ph, bPh) in enumerate(((PS0, bPS0), (PS1, bPS1))):
                                for ch in range(2):
                                    mm(Ph[:, 0:256], kcT[:, 2 * h + ch, mb * 128:(mb + 1) * 128], qcT[:, 2 * h + ch, :], ch == 0, ch == 1, [bkcT, bqcT], [bPh])
                                op("act", lambda e: e.activation(out=Pc[:, mb, :], in_=Ph[:, 0:256], func=AF.Exp, bias=nbc[:, 7:8]), [bPh, bnbc], [bPc])
                            for mb in range(2):
                                mm(PZ[:, 0:256], ones_bf[:], Pc[:, mb, :], mb == 0, mb == 1, [bPc] + C, [bPZ])
                            op("dve", lambda e: e.reciprocal(out=rz[:], in_=PZ[:, 0:256]), [bPZ], [brz])
                            for ch in range(2):
                                for mb in range(2):
                                    mm(PO_[:, 0:256], vc[:, mb, h * 256 + ch * 128:h * 256 + (ch + 1) * 128], Pc[:, mb, :], mb == 0, mb == 1, [bvc, bPc], [bPO_])
                                op("dve", lambda e: e.tensor_tensor(out=ocT[:, 2 * h + ch, :], in0=PO_[:, 0:256], in1=rz[:], op=ALU.mult), [bPO_, brz], [bocT])
                        for j in range(2):
                            row0 = tt * 256 + j * 128
                            fw.dma("sp", lambda e: e.dma_start(out=xs32[:], in_=X1D[row0:row0 + 128, :]), [bX1D], [bxs32])
                            for half, (Ph, bPh) in enumerate(((PS0, bPS0), (PS1, bPS1))):
                                for fb in range(8):
                                    mm(Ph[:], ocT[:, fb, j * 128:(j + 1) * 128], wco[:, fb, half * 512:(half + 1) * 512], fb == 0, fb == 7, [bocT, bwco], [bPh])
                                op("dve", lambda e: e.scalar_tensor_tensor(out=rres[:, half * 512:(half + 1) * 512], in0=xs32[:, half * 512:(half + 1) * 512], scalar=ALPHA, in1=Ph[:], op0=ALU.mult, op1=ALU.add), [bxs32, bPh], [brres])
                            layer_norm(rres, brres, 1, lno, blno, ltmp, bltmp, st6, bst6)
                            fw.dma("sp", lambda e: e.dma_start(out=X2D[row0:row0 + 128, :], in_=lno[:]), [blno], [bX2D])
                            if DBG:
                                fw.dma("sp", lambda e: e.dma_start(out=d_x2[row0:row0 + 128, :], in_=lno[:]), [blno], [])
                            to_featT(lno[:], blno, xfT, bxfT, row0, PT, bPT)
                    fw.barrier()

                ckpt(6)
                with ExitStack() as st:
                    wpq, bwpq = load_w_bf(st, "wpq", w_pq, 2048)
                    PA, bPA = PSB(st, "PA", [128, 512], F32)
                    PT, bPT = PSB(st, "PT", [128, 1024], BF16)
                    skT, bskT = SB(st, "skT", [128, 16, 128], BF16)
                    kbf, bkbf = SB(st, "kbf", [128, 128], BF16)
                    for hh in range(16):
                        fw.dma("sp", lambda e: e.dma_start(out=xs32[:, 0:128], in_=subk[hh]), [], [bxs32])
                        cast(kbf[:], xs32[:, 0:128], [bxs32], [bkbf])
                        tr(PT[:, 0:128], kbf[:], ident_bf[:], [bkbf] + C, [bPT])
                        op("act", lambda e: e.copy(out=skT[:, hh, :], in_=PT[:, 0:128]), [bPT], [bskT])
                    qpT, bqpT = SB(st, "qpT", [128, 16, 128], BF16)
                    S, bS = SB(st, "S", [128, 16, 128], F32)
                    wk, bwk = SB(st, "wk", [128, 256], F32)
                    m16, bm16 = SB(st, "m16", [128, 16, 16], F32)
                    i16, bi16 = SB(st, "i16", [128, 16, 16], U32)
                    i16f, bi16f = SB(st, "i16f", [128, 16, 16], F32)
                    cs_, bcs = SB(st, "cands", [128, 8, 256], F32)
                    ci_, bci = SB(st, "candi", [128, 8, 256], F32)
                    best, bbest = SB(st, "best", [128, 8, 16], F32)
                    idxf, bidxf = SB(st, "idxf", [128, 128], F32)
                    gate, bgate = SB(st, "gate", [128, 8, 16], F32)
                    g8, bg8 = SB(st, "g8", [128, 16], F32)
                    c3 = lambda ap: ap.rearrange("p (a b) -> p a b", a=16)
                    bm16h = [Buf() for _ in range(16)]
                    bi16h = [Buf() for _ in range(16)]
                    bcsh = [Buf() for _ in range(8)]
                    bcih = [Buf() for _ in range(8)]
                    bbesth = [Buf() for _ in range(8)]
                    bidxc = [Buf() for _ in range(128)]
                    wks = [SB(st, f"wk{i}", [128, 256], F32) for i in range(4)]
                    nw = 0
                    for tk in range(16):
                        tcs = slice(tk * 128, (tk + 1) * 128)
                        for q4 in range(4):
                            for hq in range(4):
                                hh = q4 * 4 + hq
                                for kc in range(8):
                                    mm(PA[:, hq * 128:(hq + 1) * 128], wpq[:, kc, hh * 128:(hh + 1) * 128], xfT[:, kc, tcs], kc == 0, kc == 7, [bwpq, bxfT], [bPA])
                            op("act", lambda e: e.copy(out=qpT[:, q4 * 4:(q4 + 1) * 4, :], in_=PA[:].rearrange("p (a b) -> p a b", a=4)), [bPA], [bqpT])
                        for q4 in range(4):
                            for hq in range(4):
                                hh = q4 * 4 + hq
                                mm(PA[:, hq * 128:(hq + 1) * 128], qpT[:, hh, :], skT[:, hh, :], True, True, [bqpT, bskT], [bPA])
                            op("act", lambda e: e.copy(out=S[:, q4 * 4:(q4 + 1) * 4, :], in_=PA[:].rearrange("p (a b) -> p a b", a=4)), [bPA], [bS])
                        for hh in range(16):
                            wk_, bwk_ = wks[nw % 4]; nw += 1
                            op("dve", lambda e: e.max(out=m16[:, hh, 0:8], in_=S[:, hh, :]), [bS], [bm16h[hh]])
                            op("dve", lambda e: e.match_replace(out=wk_[:, 0:128], in_to_replace=m16[:, hh, 0:8], in_values=S[:, hh, :], imm_value=-1e30), [bS, bm16h[hh]], [bwk_])
                            op("dve", lambda e: e.max(out=m16[:, hh, 8:16], in_=wk_[:, 0:128]), [bwk_], [bm16h[hh]])
                            op("dve", lambda e: e.max_index(out=i16[:, hh, 0:8], in_max=m16[:, hh, 0:8], in_values=S[:, hh, :]), [bS, bm16h[hh]], [bi16h[hh]])
                            op("dve", lambda e: e.max_index(out=i16[:, hh, 8:16], in_max=m16[:, hh, 8:16], in_values=S[:, hh, :]), [bS, bm16h[hh]], [bi16h[hh]])
                        op("dve", lambda e: e.tensor_copy(out=i16f[:], in_=i16[:]), bi16h, [bi16f])
                        for h in range(8):
                            op("pool", lambda e: e.tensor_tensor(out=c3(cs_[:, h, :]), in0=m16[:, 2 * h, :].unsqueeze(2).broadcast_to([128, 16, 16]), in1=m16[:, 2 * h + 1, :].unsqueeze(1).broadcast_to([128, 16, 16]), op=ALU.add), [bm16h[2 * h], bm16h[2 * h + 1]], [bcsh[h]])
                            op("dve", lambda e: e.scalar_tensor_tensor(out=c3(ci_[:, h, :]), in0=i16f[:, 2 * h, :].unsqueeze(2).broadcast_to([128, 16, 16]), scalar=128.0, in1=i16f[:, 2 * h + 1, :].unsqueeze(1).broadcast_to([128, 16, 16]), op0=ALU.mult, op1=ALU.add), [bi16f], [bcih[h]])
                        for h in range(8):
                            wk_, bwk_ = wks[nw % 4]; nw += 1
                            op("dve", lambda e: e.max(out=best[:, h, 0:8], in_=cs_[:, h, :]), [bcsh[h]], [bbesth[h]])
                            op("dve", lambda e: e.match_replace(out=wk_[:], in_to_replace=best[:, h, 0:8], in_values=cs_[:, h, :], imm_value=-1e30), [bcsh[h], bbesth[h]], [bwk_])
                            op("dve", lambda e: e.max(out=best[:, h, 8:16], in_=wk_[:]), [bwk_], [bbesth[h]])
                        op("pool", lambda e: e.memset(idxf[:], 0.0), [], bidxc + [bidxf])
                        for h in range(8):
                            for k in range(16):
                                op("dve", lambda e: e.scalar_tensor_tensor(out=wk[:], in0=cs_[:, h, :], scalar=best[:, h, k:k + 1], in1=ci_[:, h, :], op0=ALU.is_equal, op1=ALU.mult, accum_out=idxf[:, h * 16 + k:h * 16 + k + 1]), [bcsh[h], bcih[h], bbesth[h]], [bidxc[h * 16 + k]])
                        bbest_all = bbesth
                        op("dve", lambda e: e.tensor_scalar_min(out=idxf[:], in0=idxf[:], scalar1=16383.0), bidxc + [bidxf], [bidxf])
                        op("dve", lambda e: e.tensor_tensor(out=gate[:], in0=best[:], in1=best[:, :, 0:1].broadcast_to([128, 8, 16]), op=ALU.subtract), bbesth, [bgate])
                        op("act", lambda e: e.activation(out=gate[:], in_=gate[:], func=AF.Exp), [bgate], [bgate])
                        op("dve", lambda e: e.reduce_sum(out=g8[:, 0:8], in_=gate[:], axis=AX.X), [bgate], [bg8])
                        op("dve", lambda e: e.reciprocal(out=g8[:, 8:16], in_=g8[:, 0:8]), [bg8], [bg8])
                        op("dve", lambda e: e.tensor_tensor(out=gate[:], in0=gate[:], in1=g8[:, 8:16].unsqueeze(2).broadcast_to([128, 8, 16]), op=ALU.mult), [bgate, bg8], [bgate])
                        op("pe", lambda e: e.transpose(out=PA[:, 0:128], in_=idxf[:], identity=ident_f[:]), [bidxf] + C, [bPA], skip_self=True)
                        op("pe", lambda e: e.transpose(out=PA[:, 128:256], in_=gate[:].rearrange("p a b -> p (a b)"), identity=ident_f[:]), [bgate] + C, [bPA], skip_self=True)
                        op("dve", lambda e: e.tensor_copy(out=idxT_t[:], in_=PA[:, 0:128]), [bPA], [bidxT])
                        op("dve", lambda e: e.tensor_copy(out=gateT_t[:], in_=PA[:, 128:256]), [bPA], [bgateT])
                        fw.dma("sp", lambda e: e.dma_start(out=IDXD[:, tcs], in_=idxT_t[:]), [bidxT], [bIDXD])
                        fw.dma("sp", lambda e: e.dma_start(out=GATED[:, tcs], in_=gateT_t[:]), [bgateT], [bGATED])
                    fw.barrier()

            ckpt(7)
            with ExitStack() as st:
                load_ln(2)
                PA, bPA = PSB(st, "PA", [128, 512], F32)
                PXB = [PSB(st, f"PXB{i}", [128, 1024], F32) for i in range(2)]
                PYT, bPYT = PSB(st, "PYT", [128, 1024], F32)
                RS, bRS = SB(st, "RS", [128, 128, 128], BF16)
                op("pool", lambda e: e.affine_select(out=RS[:], in_=ones_f[:].unsqueeze(1).broadcast_to([128, 128, 128]), pattern=[[-1, 128], [0, 128]], compare_op=ALU.is_equal, fill=0.0, base=0, channel_multiplier=1), C, [bRS])
                hidT, _ = SB(st, "hidT", [128, 128], F32)
                gcol, _ = SB(st, "gcol", [128, 128], F32)
                Rc, bRc = SB(st, "Rc", [128, 255], BF16)
                op("pool", lambda e: e.memset(Rc[:], 0.0), [], [bRc])
                op("pool", lambda e: e.memset(Rc[:, 127:128], 1.0), [bRc], [bRc])
                Lt = [SB(st, f"Lt{i}", [128, 128], BF16) for i in range(4)]
                bhid = [Buf() for _ in range(128)]
                bgc = [Buf() for _ in range(128)]
                bwc = [Buf() for _ in range(128)]
                x2b, bx2b = SB(st, "x2b", [128, 1024], BF16)
                NG = 8
                UVg = [SB(st, f"UVg{i}", [128, 2048], BF16) for i in range(NG)]
                junk2, bjunk2 = SB(st, "junk2", [128, 1024], F32)
                yT, byT = SB(st, "yT", [128, 8, 128], F32)
                print("sbuf remaining (2C-2):", nc.sbuf_bytes_remaining)
                for tk in range(16):
                    fw.dma("sp", lambda e: e.dma_start(out=xs32[:], in_=X2D[tk * 128:(tk + 1) * 128, :]), [bX2D], [bxs32])
                    cast(x2b[:], xs32[:], [bxs32], [bx2b])
                    fw.dma("sp", lambda e: e.dma_start(out=idxT_t[:], in_=IDXD[:, tk * 128:(tk + 1) * 128]), [bIDXD], [bidxT])
                    fw.dma("sp", lambda e: e.dma_start(out=gateT_t[:], in_=GATED[:, tk * 128:(tk + 1) * 128]), [bGATED], [bgateT])
                    op("dve", lambda e: e.memset(hidT[:], 0.0), [], bhid)
                    for tt_ in range(128 + 2):
                        if tt_ < 128:
                            t = tt_
                            UV_, bUV_ = UVg[t % NG]
                            Px, bPx = PXB[t % 2]
                            fw.dma("pool", lambda e: e.indirect_dma_start(out=UV_[:], out_offset=None, in_=UVB, in_offset=bass.IndirectOffsetOnAxis(ap=idxT_t[:, t:t + 1], axis=0)), [bidxT, bUVB], [bUV_])
                            for half in range(2):
                                mm(Px[:, half * 512:(half + 1) * 512], RS[:, t, :], x2b[:, half * 512:(half + 1) * 512], True, True, [bRS, bx2b], [bPx])
                            op("dve", lambda e: e.scalar_tensor_tensor(out=junk2[:], in0=UV_[:, 0:1024], scalar=1.0, in1=Px[:], op0=ALU.mult, op1=ALU.mult, accum_out=hidT[:, t:t + 1]), [bUV_, bPx], [bjunk2, bhid[t]])
                            op("act", lambda e: e.activation(out=gcol[:, t:t + 1], in_=hidT[:, t:t + 1], func=AF.Gelu), [bhid[t]], [bgc[t]])
                        if 0 <= tt_ - 1 < 128:
                            t = tt_ - 1
                            L_, bL_ = Lt[t % 4]
                            op("dve", lambda e: e.tensor_scalar(out=L_[:], in0=Rc[:, 127 - t:255 - t], scalar1=gcol[:, t:t + 1], scalar2=gateT_t[:, t:t + 1], op0=ALU.mult, op1=ALU.mult), [bRc, bgc[t], bgateT], [bL_])
                        if 0 <= tt_ - 2 < 128:
                            t = tt_ - 2
                            UV_, bUV_ = UVg[t % NG]
                            L_, bL_ = Lt[t % 4]
                            for half in range(2):
                                mm(PYT[:, half * 512:(half + 1) * 512], L_[:], UV_[:, 1024 + half * 512:1024 + (half + 1) * 512], t == 0, t == 127, [bUV_, bL_], [bPYT])
                    op("dve", lambda e: e.scalar_tensor_tensor(out=rres[:], in0=xs32[:], scalar=ALPHA, in1=PYT[:], op0=ALU.mult, op1=ALU.add), [bxs32, bPYT], [brres])
                    layer_norm(rres, brres, 2, lno, blno, ltmp, bltmp, st6, bst6)
                    fw.dma("sp", lambda e: e.dma_start(out=out[tk * 128:(tk + 1) * 128, :], in_=lno[:]), [blno], [])
                fw.barrier()
    return nc


_NC = {}


def _prep(inputs):
    x = np.asarray(inputs["x"], np.float32); mem = np.asarray(inputs["mem"], np.float32)
    w_in = np.asarray(inputs["w_in"][0], np.float32)
    conv_w = np.asarray(inputs["conv_w"][0], np.float32); conv_b = np.asarray(inputs["conv_b"][0], np.float32)
    rep = lambda v: np.ascontiguousarray(np.broadcast_to(np.asarray(v, np.float32).reshape(1, -1), (128, np.asarray(v).size)))
    lnp = np.stack([rep(inputs[k][0]) for k in ("ln1_g", "ln1_b", "ln2_g", "ln2_b", "ln3_g", "ln3_b")], axis=1)
    gb = np.asarray(inputs["gate_bias"][0], np.float32).reshape(16, 128).T
    shared = dict(
        wg=np.ascontiguousarray(w_in[:, 8224:10272]), lnp=np.ascontiguousarray(lnp), gbias=np.ascontiguousarray(gb),
        w_ssd_br=np.asarray(inputs["w_ssd_br"][0], np.float32), w_diff_br=np.asarray(inputs["w_diff_br"][0], np.float32),
        w_o=np.asarray(inputs["w_o"][0], np.float32), w_cq=np.asarray(inputs["w_cq"][0], np.float32),
        w_ck=np.asarray(inputs["w_ck"][0], np.float32), w_cv=np.asarray(inputs["w_cv"][0], np.float32),
        w_co=np.asarray(inputs["w_co"][0], np.float32), w_pq=np.asarray(inputs["w_pq"][0], np.float32),
        sub_keys=np.ascontiguousarray(np.asarray(inputs["sub_keys"][0], np.float32).reshape(16, 128, 128)),
        peer_uv=np.ascontiguousarray(np.concatenate([np.asarray(inputs["peer_u"][0], np.float32), np.asarray(inputs["peer_v"][0], np.float32)], axis=1)),
    )
    maps = []
    for core in range(8):
        b, c = core // 4, core % 4
        cols = np.concatenate([
            2048 + c * 512 + np.arange(512),
            2048 + 2048 + c * 128 + np.arange(128),
            2048 + 2048 + 512 + c * 128 + np.arange(128),
            c * 512 + np.arange(512),
            5120 + c * 8 + np.arange(8),
            5152 + c * 256 + np.arange(256),
            6176 + c * 256 + np.arange(256),
            7200 + c * 256 + np.arange(256),
        ])
        ccols = np.concatenate([c * 512 + np.arange(512), 2048 + c * 128 + np.arange(128), 2048 + 512 + c * 128 + np.arange(128)])
        cwc = conv_w[:, ccols].T.reshape(6, 128, 4).transpose(1, 0, 2)
        cbc = conv_b[ccols].reshape(6, 128).T
        hs = slice(c * 8, (c + 1) * 8)
        rowp = np.concatenate([
            rep(inputs["dt_bias"][0][hs]), rep(inputs["a_log"][0][hs]), rep(inputs["d_skip"][0][hs]),
            rep(inputs["ssd_norm_w"][0][c * 512:(c + 1) * 512]), rep(inputs["subln_w"][0]),
            rep(np.asarray(inputs["lam_q"][0]).reshape(-1)), rep(np.asarray(inputs["lam_k"][0]).reshape(-1))], axis=1)
        assert rowp.shape == (128, NR)
        p = np.arange(128)[:, None, None]; kb = np.arange(24)[None, :, None]; tt = np.arange(8)[None, None, :]
        row = (kb % 6) * 128 + p
        R = (row // 64) * 256 + (kb // 6) * 64 + (row % 64)
        idq = ((R * 4 + c) * 8 + tt).astype(np.int32)
        m = dict(shared)
        m.update(xb=np.ascontiguousarray(x[b]), xown=np.ascontiguousarray(x[b, c * 2048:(c + 1) * 2048]),
                 memb=np.ascontiguousarray(mem[b]), w1=np.ascontiguousarray(w_in[:, cols]),
                 cw=np.ascontiguousarray(cwc), cbv=np.ascontiguousarray(cbc), rowp=np.ascontiguousarray(rowp),
                 idq=np.ascontiguousarray(idq))
        maps.append(m)
    return maps


def run(inputs, DBG=False, stop=99):
    key = (bool(DBG), stop)
    if key not in _NC:
        _NC[key] = build(DBG, stop)
    maps = _prep(inputs)
    res = run_bass_kernel_spmd(_NC[key], maps, core_ids=list(range(8)))
    return res.results


def kernel(**inputs):
    R = run(inputs, False)
    out = np.empty((2, 8192, 1024), np.float32)
    for core in range(8):
        b, c = core // 4, core % 4
        out[b, c * 2048:(c + 1) * 2048] = R[core]["out"]
    return out
```
